# Optimizing a Trainium2 kernel written in Bass

```python
import jax, jax.numpy as jnp
from jax import lax
import numpy as np

D_MODEL = 1024
BATCH = 8
SEQ = 2048
DEPTH = 2
DEC_BATCH = 32
DEC_SEQ = 4
PAST_LEN = 8192
PAGE_SIZE = 128

GROUP_WIDTH = D_MODEL // 2
MIX_WIDTH = 4 * GROUP_WIDTH
CONV_W = 4
GDN_HEADS = 4
GDN_DK = GROUP_WIDTH // GDN_HEADS
GDN_DV = GDN_DK
GDN_CHUNK = 64
SSD_HEADS = 8
SSD_P = GROUP_WIDTH // SSD_HEADS
SSD_GROUPS = 2
SSD_N = 64
SSD_CHUNK = 64
FOX_HEADS = 8
FOX_DH = GROUP_WIDTH // FOX_HEADS
FOX_BLOCK = 128
MEM_LEN = 256
MEM_HEADS = 4
MEM_DH = GROUP_WIDTH // MEM_HEADS

GDN_QKV = 3 * GROUP_WIDTH
SSD_XBC = GROUP_WIDTH + 2 * SSD_GROUPS * SSD_N
FOX_QKV = 3 * GROUP_WIDTH
IN_SPLITS = (GDN_QKV, GDN_HEADS, GDN_HEADS, GROUP_WIDTH,
             SSD_XBC, SSD_HEADS, GROUP_WIDTH,
             FOX_QKV, FOX_HEADS, GROUP_WIDTH,
             GROUP_WIDTH, GROUP_WIDTH)
N_IN = sum(IN_SPLITS)
DEEPNORM_ALPHA = (2 * DEPTH) ** 0.25
DEEPNORM_BETA = (8 * DEPTH) ** -0.25
EPS = 1e-6
NEG_BIG = -1e30

kernel_name = "hymba_gdn_ssd_fox_memory_deepnorm_step"


def _split_cols(t, sizes):
    idx = np.cumsum(sizes)[:-1].tolist()
    return jnp.split(t, idx, axis=-1)


def _layer_norm(x, g, b):
    xf = x.astype(jnp.float32)
    mu = jnp.mean(xf, -1, keepdims=True)
    var = jnp.mean(jnp.square(xf - mu), -1, keepdims=True)
    return ((xf - mu) * lax.rsqrt(var + EPS) * g + b).astype(x.dtype)


def _rms_norm(x, g):
    xf = x.astype(jnp.float32)
    return xf * lax.rsqrt(jnp.mean(jnp.square(xf), -1, keepdims=True) + EPS) * g


def _l2norm(x):
    xf = x.astype(jnp.float32)
    return xf * lax.rsqrt(jnp.sum(jnp.square(xf), -1, keepdims=True) + EPS)


def _causal_conv(u, buf, w, b):
    L = u.shape[1]
    cat = jnp.concatenate([buf.astype(u.dtype), u], axis=1)
    y = b + sum(cat[:, i:i + L] * w[i] for i in range(CONV_W))
    return jax.nn.silu(y), cat[:, -(CONV_W - 1):]


def _to_chunks(t, c):
    B, L = t.shape[:2]
    pad = (-L) % c
    t = jnp.pad(t, [(0, 0), (0, pad)] + [(0, 0)] * (t.ndim - 2))
    t = t.reshape((B, (L + pad) // c, c) + t.shape[2:])
    return jnp.moveaxis(t, 2, 3)


def _from_chunks(t, L):
    N, B, H, C, D = t.shape
    return t.transpose(1, 0, 3, 2, 4).reshape(B, N * C, H, D)[:, :L]


def _decay_matrix(gam):
    C = gam.shape[-1]
    tril = jnp.tril(jnp.ones((C, C), dtype=bool))
    diff = gam[..., :, None] - gam[..., None, :]
    return jnp.where(tril, jnp.exp(jnp.where(tril, diff, 0.0)), 0.0)


def _gated_delta_rule(q, k, v, g, beta, S0):
    f32 = jnp.float32
    L = q.shape[1]
    c = min(GDN_CHUNK, L)
    qc, kc, vc = (_to_chunks(t.astype(f32), c) for t in (q, k, v))
    gc, bc = (_to_chunks(t.astype(f32), c) for t in (g, beta))
    gam = jnp.cumsum(gc, axis=-1)
    decay = _decay_matrix(gam)
    strict = jnp.tril(jnp.ones((c, c), dtype=bool), -1)
    a = jnp.where(strict, bc[..., :, None] * jnp.einsum('bnhid,bnhjd->bnhij', kc, kc) * decay, 0.0)
    rhs = jnp.concatenate([vc * bc[..., None], kc * (bc * jnp.exp(gam))[..., None]], axis=-1)
    sol = lax.linalg.triangular_solve(jnp.eye(c, dtype=f32) + a, rhs, left_side=True, lower=True)
    dv = v.shape[-1]
    w_val, w_key = sol[..., :dv], sol[..., dv:]
    qk = jnp.einsum('bnhid,bnhjd->bnhij', qc, kc) * decay
    q_dec = qc * jnp.exp(gam)[..., None]
    k_tail = kc * jnp.exp(gam[..., -1:] - gam)[..., None]
    chunk_dec = jnp.exp(gam[..., -1])

    def step(S, xs):
        wv, wk, qk_n, qd, kt, cd = xs
        u = wv - jnp.einsum('bhck,bhkv->bhcv', wk, S)
        o = jnp.einsum('bhck,bhkv->bhcv', qd, S) + jnp.einsum('bhij,bhjv->bhiv', qk_n, u)
        S = S * cd[..., None, None] + jnp.einsum('bhck,bhcv->bhkv', kt, u)
        return S, o

    xs = tuple(jnp.moveaxis(t, 1, 0) for t in (w_val, w_key, qk, q_dec, k_tail, chunk_dec))
    S, o = lax.scan(step, S0.astype(f32), xs)
    return _from_chunks(o, L), S


def _ssd_scan(x, dt, A, Bm, Cm, h0):
    f32 = jnp.float32
    L, H, G = x.shape[1], x.shape[2], Bm.shape[2]
    c = min(SSD_CHUNK, L)
    Bh = jnp.repeat(Bm.astype(f32), H // G, axis=2)
    Ch = jnp.repeat(Cm.astype(f32), H // G, axis=2)
    xdt = x.astype(f32) * dt[..., None]
    xc, bc, cc = (_to_chunks(t, c) for t in (xdt, Bh, Ch))
    gam = jnp.cumsum(_to_chunks(dt * A, c), axis=-1)
    decay = _decay_matrix(gam)
    cb = jnp.einsum('bnhis,bnhjs->bnhij', cc, bc) * decay
    y_intra = jnp.einsum('bnhij,bnhjp->bnhip', cb, xc)
    c_dec = cc * jnp.exp(gam)[..., None]
    b_tail = bc * jnp.exp(gam[..., -1:] - gam)[..., None]
    chunk_dec = jnp.exp(gam[..., -1])

    def step(h, xs):
        yi, cdn, bt, xn, dec = xs
        y = yi + jnp.einsum('bhcs,bhps->bhcp', cdn, h)
        h = h * dec[..., None, None] + jnp.einsum('bhcp,bhcs->bhps', xn, bt)
        return h, y

    xs = tuple(jnp.moveaxis(t, 1, 0) for t in (y_intra, c_dec, b_tail, xc, chunk_dec))
    h, y = lax.scan(step, h0.astype(f32), xs)
    return _from_chunks(y, L), h


def _fox_block(q, k, v, cq, ck, q_pos, k_pos):
    s = jnp.einsum('bqhd,bkhd->bhqk', q, k).astype(jnp.float32) * FOX_DH ** -0.5
    s = s + jnp.swapaxes(cq, 1, 2)[..., :, None] - jnp.swapaxes(ck, 1, 2)[..., None, :]
    s = jnp.where(k_pos[None, :] <= q_pos[:, None], s, NEG_BIG)
    p = jax.nn.softmax(s, axis=-1)
    return jnp.einsum('bhqk,bkhd->bqhd', p.astype(v.dtype), v)


def _forgetting_attention(q, k, v, cq, ck, q_pos, k_pos):
    B, L, H, D = q.shape
    qb = min(FOX_BLOCK, L)
    pad = (-L) % qb
    nb = (L + pad) // qb

    def blocks(t):
        t = jnp.pad(t, [(0, 0), (0, pad)] + [(0, 0)] * (t.ndim - 2))
        return jnp.moveaxis(t.reshape((B, nb, qb) + t.shape[2:]), 1, 0)

    pos_b = jnp.pad(q_pos, (0, pad), mode='edge').reshape(nb, qb)
    out = lax.map(lambda t: _fox_block(t[0], k, v, t[1], ck, t[2], k_pos), (blocks(q), blocks(cq), pos_b))
    return jnp.moveaxis(out, 0, 1).reshape(B, nb * qb, H, D)[:, :L]


def _memory_attention(q, mk, mv):
    s = jnp.einsum('blhd,bmhd->bhlm', q, mk).astype(jnp.float32) * MEM_DH ** -0.5
    p = jax.nn.softmax(s, axis=-1)
    return jnp.einsum('bhlm,bmhd->blhd', p.astype(mv.dtype), mv)


def _hybrid_layer(x, mem_k, mem_v, gdn_conv0, gdn_s0, ssd_conv0, ssd_h0,
                  fox_k_past, fox_v_past, fox_logf_past, p):
    f32 = jnp.float32
    B, L, _ = x.shape
    past = 0 if fox_k_past is None else fox_k_past.shape[1]
    proj = jnp.einsum('bld,de->ble', x, p['w_in'])
    (a_qkv, a_beta, a_alpha, a_gate, b_xbc, b_dt, b_gate,
     c_qkv, c_f, c_gate, d_q, d_gate) = _split_cols(proj, IN_SPLITS)

    a_qkv, gdn_conv_new = _causal_conv(a_qkv, gdn_conv0, p['gdn_conv_w'], p['gdn_conv_b'])
    qa, ka, va = (t.reshape(B, L, GDN_HEADS, GDN_DK) for t in jnp.split(a_qkv, 3, axis=-1))
    beta = jax.nn.sigmoid(a_beta.astype(f32))
    g = -jnp.exp(p['gdn_a_log'].astype(f32)) * jax.nn.softplus(a_alpha.astype(f32) + p['gdn_dt_bias'])
    o_a, gdn_s_new = _gated_delta_rule(_l2norm(qa) * GDN_DK ** -0.5, _l2norm(ka), va, g, beta, gdn_s0)
    y_a = _rms_norm(o_a, p['gdn_norm_w']).reshape(B, L, GROUP_WIDTH).astype(x.dtype) * jax.nn.silu(a_gate)

    b_xbc, ssd_conv_new = _causal_conv(b_xbc, ssd_conv0, p['ssd_conv_w'], p['ssd_conv_b'])
    xs, bm, cm = _split_cols(b_xbc, (GROUP_WIDTH, SSD_GROUPS * SSD_N, SSD_GROUPS * SSD_N))
    xs = xs.reshape(B, L, SSD_HEADS, SSD_P)
    dt = jax.nn.softplus(b_dt.astype(f32) + p['ssd_dt_bias'])
    y_b, ssd_h_new = _ssd_scan(xs, dt, -jnp.exp(p['ssd_a_log'].astype(f32)),
                               bm.reshape(B, L, SSD_GROUPS, SSD_N), cm.reshape(B, L, SSD_GROUPS, SSD_N), ssd_h0)
    y_b = y_b + p['ssd_d'][:, None] * xs
    y_b = _rms_norm(y_b.reshape(B, L, GROUP_WIDTH) * jax.nn.silu(b_gate.astype(f32)), p['ssd_norm_w']).astype(x.dtype)

    qc, kc, vc = (t.reshape(B, L, FOX_HEADS, FOX_DH) for t in jnp.split(c_qkv, 3, axis=-1))
    logf = jax.nn.log_sigmoid(c_f.astype(f32) + p['fox_f_bias'])
    if fox_k_past is None:
        k_all, v_all, logf_all = kc, vc, logf
    else:
        k_all = jnp.concatenate([fox_k_past.astype(kc.dtype), kc], axis=1)
        v_all = jnp.concatenate([fox_v_past.astype(vc.dtype), vc], axis=1)
        logf_all = jnp.concatenate([fox_logf_past.astype(f32), logf], axis=1)
    c_all = jnp.cumsum(logf_all, axis=1)
    o_c = _forgetting_attention(qc, k_all, v_all, c_all[:, past:], c_all,
                                past + jnp.arange(L), jnp.arange(past + L))
    y_c = o_c.reshape(B, L, GROUP_WIDTH) * jax.nn.silu(c_gate)

    o_d = _memory_attention(d_q.reshape(B, L, MEM_HEADS, MEM_DH), mem_k, mem_v)
    y_d = o_d.reshape(B, L, GROUP_WIDTH) * jax.nn.silu(d_gate)

    mixed = jnp.concatenate([y_a, y_b, y_c, y_d], axis=-1)
    y = jnp.einsum('ble,ed->bld', mixed, p['w_out'])
    x_new = _layer_norm(DEEPNORM_ALPHA * x + y, p['ln_g'], p['ln_b'])
    return (x_new, gdn_conv_new, gdn_s_new.astype(gdn_s0.dtype), ssd_conv_new,
            ssd_h_new.astype(ssd_h0.dtype), kc, vc, logf.astype(x.dtype))


def setup_inputs(seed: int = 0) -> dict:
    key = jax.random.key(seed)
    ks = iter(jax.random.split(key, 48))
    f32 = jnp.float32

    def nrm(shape, scale=1.0):
        return scale * jax.random.normal(next(ks), shape, f32)

    def log_uniform_a(shape):
        return jnp.log(jax.random.uniform(next(ks), shape, f32, 1.0, 16.0))

    def dt_bias(shape):
        dt = jnp.exp(jax.random.uniform(next(ks), shape, f32, np.log(1e-3), np.log(1e-1)))
        return dt + jnp.log(-jnp.expm1(-dt))

    n_pages = PAST_LEN // PAGE_SIZE
    n_used = DEC_BATCH * n_pages
    n_pool = n_used + n_used // 4
    page_table = jax.random.permutation(next(ks), n_pool)[:n_used].reshape(DEC_BATCH, n_pages).astype(jnp.int32)

    return {
        "x_prompt": nrm((BATCH, SEQ, D_MODEL)),
        "x_sample": nrm((DEC_BATCH, DEC_SEQ, D_MODEL)),
        "state_gdn_conv": nrm((DEPTH, DEC_BATCH, CONV_W - 1, GDN_QKV)),
        "state_gdn": nrm((DEPTH, DEC_BATCH, GDN_HEADS, GDN_DK, GDN_DV), GDN_DK ** -0.5),
        "state_ssd_conv": nrm((DEPTH, DEC_BATCH, CONV_W - 1, SSD_XBC)),
        "state_ssd": nrm((DEPTH, DEC_BATCH, SSD_HEADS, SSD_P, SSD_N), 0.5),
        "cache_fox_k": nrm((DEPTH, n_pool, PAGE_SIZE, FOX_HEADS, FOX_DH)),
        "cache_fox_v": nrm((DEPTH, n_pool, PAGE_SIZE, FOX_HEADS, FOX_DH)),
        "cache_fox_logf": jax.nn.log_sigmoid(3.0 + nrm((DEPTH, n_pool, PAGE_SIZE, FOX_HEADS))),
        "cache_mem_k": nrm((DEPTH, DEC_BATCH, MEM_LEN, MEM_HEADS, MEM_DH)),
        "cache_mem_v": nrm((DEPTH, DEC_BATCH, MEM_LEN, MEM_HEADS, MEM_DH)),
        "page_table": page_table,
        "mem_prompt": nrm((BATCH, MEM_LEN, D_MODEL)),
        "w_in": nrm((DEPTH, D_MODEL, N_IN), D_MODEL ** -0.5),
        "gdn_conv_w": nrm((DEPTH, CONV_W, GDN_QKV), CONV_W ** -0.5),
        "gdn_conv_b": nrm((DEPTH, GDN_QKV), 0.02),
        "gdn_a_log": log_uniform_a((DEPTH, GDN_HEADS)),
        "gdn_dt_bias": dt_bias((DEPTH, GDN_HEADS)),
        "gdn_norm_w": 1.0 + nrm((DEPTH, GDN_DV), 0.02),
        "ssd_conv_w": nrm((DEPTH, CONV_W, SSD_XBC), CONV_W ** -0.5),
        "ssd_conv_b": nrm((DEPTH, SSD_XBC), 0.02),
        "ssd_a_log": log_uniform_a((DEPTH, SSD_HEADS)),
        "ssd_dt_bias": dt_bias((DEPTH, SSD_HEADS)),
        "ssd_d": 1.0 + nrm((DEPTH, SSD_HEADS), 0.1),
        "ssd_norm_w": 1.0 + nrm((DEPTH, GROUP_WIDTH), 0.02),
        "fox_f_bias": 3.0 + nrm((DEPTH, FOX_HEADS), 0.1),
        "w_mem_kv": nrm((DEPTH, D_MODEL, 2 * GROUP_WIDTH), D_MODEL ** -0.5),
        "w_out": nrm((DEPTH, MIX_WIDTH, D_MODEL), DEEPNORM_BETA * MIX_WIDTH ** -0.5),
        "ln_g": 1.0 + nrm((DEPTH, D_MODEL), 0.02),
        "ln_b": nrm((DEPTH, D_MODEL), 0.02),
    }


def reference(x_prompt, x_sample, state_gdn_conv, state_gdn, state_ssd_conv, state_ssd,
              cache_fox_k, cache_fox_v, cache_fox_logf, cache_mem_k, cache_mem_v, page_table,
              mem_prompt, w_in, gdn_conv_w, gdn_conv_b, gdn_a_log, gdn_dt_bias, gdn_norm_w,
              ssd_conv_w, ssd_conv_b, ssd_a_log, ssd_dt_bias, ssd_d, ssd_norm_w, fox_f_bias,
              w_mem_kv, w_out, ln_g, ln_b):
    dt_ = x_prompt.dtype
    bp, lp = x_prompt.shape[:2]
    bs = x_sample.shape[0]
    past_len = page_table.shape[1] * cache_fox_k.shape[2]
    names = ('gdn_conv', 'gdn_state', 'ssd_conv', 'ssd_state', 'fox_k', 'fox_v', 'fox_logf')
    pn = {n: [] for n in names + ('mem_k', 'mem_v')}
    sn = {n: [] for n in names}
    xp, xs = x_prompt, x_sample
    for l in range(DEPTH):
        p = dict(w_in=w_in[l], gdn_conv_w=gdn_conv_w[l], gdn_conv_b=gdn_conv_b[l],
                 gdn_a_log=gdn_a_log[l], gdn_dt_bias=gdn_dt_bias[l], gdn_norm_w=gdn_norm_w[l],
                 ssd_conv_w=ssd_conv_w[l], ssd_conv_b=ssd_conv_b[l], ssd_a_log=ssd_a_log[l],
                 ssd_dt_bias=ssd_dt_bias[l], ssd_d=ssd_d[l], ssd_norm_w=ssd_norm_w[l],
                 fox_f_bias=fox_f_bias[l], w_out=w_out[l], ln_g=ln_g[l], ln_b=ln_b[l])
        mkv = jnp.einsum('bmd,de->bme', mem_prompt, w_mem_kv[l])
        mk, mv = (t.reshape(bp, mem_prompt.shape[1], MEM_HEADS, MEM_DH) for t in jnp.split(mkv, 2, axis=-1))
        out_p = _hybrid_layer(
            xp, mk, mv,
            jnp.zeros((bp, CONV_W - 1, GDN_QKV), dt_), jnp.zeros((bp, GDN_HEADS, GDN_DK, GDN_DV), dt_),
            jnp.zeros((bp, CONV_W - 1, SSD_XBC), dt_), jnp.zeros((bp, SSD_HEADS, SSD_P, SSD_N), dt_),
            None, None, None, p)
        xp = out_p[0]
        for n, t in zip(names, out_p[1:]):
            pn[n].append(t)
        pn['mem_k'].append(mk)
        pn['mem_v'].append(mv)
        fk = cache_fox_k[l][page_table].reshape(bs, past_len, FOX_HEADS, FOX_DH)
        fv = cache_fox_v[l][page_table].reshape(bs, past_len, FOX_HEADS, FOX_DH)
        fl = cache_fox_logf[l][page_table].reshape(bs, past_len, FOX_HEADS)
        out_s = _hybrid_layer(
            xs, cache_mem_k[l], cache_mem_v[l],
            state_gdn_conv[l], state_gdn[l], state_ssd_conv[l], state_ssd[l],
            fk, fv, fl, p)
        xs = out_s[0]
        for n, t in zip(names, out_s[1:]):
            sn[n].append(t)
    P = {n: jnp.stack(v) for n, v in pn.items()}
    S = {n: jnp.stack(v) for n, v in sn.items()}
    return (xp, xs,
            P['gdn_conv'], P['gdn_state'], P['ssd_conv'], P['ssd_state'],
            P['fox_k'], P['fox_v'], P['fox_logf'], P['mem_k'], P['mem_v'],
            S['gdn_conv'], S['gdn_state'], S['ssd_conv'], S['ssd_state'],
            S['fox_k'], S['fox_v'], S['fox_logf'])
```

```python
import numpy as np
import concourse.bass as bass
import concourse.mybir as mybir

F32 = mybir.dt.float32
BF16 = mybir.dt.bfloat16
I32 = mybir.dt.int32
ALU = mybir.AluOpType
AF = mybir.ActivationFunctionType
AX = mybir.AxisListType

ENGS = ("pe", "act", "dve", "pool", "sp")
import os as _os
NDMA_SEM = int(_os.environ.get('FW_NDMA', '24'))


class V:
    __slots__ = ("ap", "keys")

    def __init__(self, ap, keys):
        self.ap = ap
        self.keys = tuple(keys)


class Buf:
    def __init__(self, name, t, nsub=0):
        self.name = name
        self.t = t

    def __getitem__(self, idx):
        return V(self.t[idx], ((self.name,),))

    def k(self, *sub):
        return _Sub(self, sub)


class _Sub:
    def __init__(self, buf, sub):
        self.buf = buf
        self.sub = sub

    def __getitem__(self, idx):
        return V(self.buf.t[idx], ((self.buf.name,) + tuple(self.sub),))


class Op:
    __slots__ = ("eng", "fn", "reads", "writes", "kind", "deps", "sig", "dma_sem", "dma_val", "idx", "dma_prev")

    def __init__(self, eng, fn, reads, writes, kind):
        self.eng = eng
        self.fn = fn
        self.reads = reads
        self.writes = writes
        self.kind = kind
        self.deps = []
        self.sig = None
        self.dma_sem = None
        self.dma_val = None
        self.dma_prev = None


class Prog:
    def __init__(self, nc, stack):
        self.nc = nc
        self.stack = stack
        self.ops = []
        self.last_w = {}
        self.readers = {}
        self.nbuf = 0
        self.psum_rr = 0

    def sb(self, name, shape, dt=F32):
        t = self.stack.enter_context(self.nc.sbuf_tensor(name, list(shape), dt))
        return Buf(name, t)

    def ps(self, name, shape=(128, 512), dt=F32):
        t = self.stack.enter_context(self.nc.psum_tensor(name, list(shape), dt))
        return Buf(name, t)

    def op(self, eng, fn, w=(), r=(), kind="c"):
        reads = []
        for v in r:
            reads.extend(v.keys)
        writes = []
        for v in w:
            writes.extend(v.keys)
        writes = writes + [k for k in reads if k[0].startswith("ps") and k not in writes]
        o = Op(eng, fn, reads, writes, kind)
        o.idx = len(self.ops)
        deps = set()
        for k in _expand(reads):
            lw = self.last_w.get(k)
            if lw is not None:
                deps.add((lw, "raw"))
        for k in _expand(writes):
            lw = self.last_w.get(k)
            if lw is not None:
                deps.add((lw, "waw"))
            for rd in self.readers.get(k, ()):
                deps.add((rd, "war"))
        for k in reads:
            self.readers.setdefault(k, []).append(o.idx)
        for k in writes:
            self.last_w[k] = o.idx
            self.readers[k] = []
        o.deps = [(d, t) for (d, t) in deps if d != o.idx]
        self.ops.append(o)
        return o

    def mm(self, out, lhsT, rhs, start=True, stop=True, extra_r=(), **kw):
        return self.op("pe", lambda e: e.matmul(out.ap, lhsT.ap, rhs.ap, start=start, stop=stop, **kw),
                       w=[out], r=[lhsT, rhs] + list(extra_r), kind="mm")

    def tr(self, out, in_, ident):
        return self.op("pe", lambda e: e.transpose(out.ap, in_.ap, ident.ap), w=[out], r=[in_, ident], kind="mm")

    def dma(self, out, in_, eng="sp", **kw):
        return self.op(eng, lambda e: e.dma_start(out=out.ap, in_=in_.ap, **kw), w=[out], r=[in_], kind="d")

    def emit(self):
        nc = self.nc
        ops = self.ops
        need_sig = set()
        for o in ops:
            for (d, t) in o.deps:
                p = ops[d]
                if p.kind == "d":
                    continue
                if p.eng == o.eng and o.kind != "d":
                    if p.eng == "pe":
                        continue
                need_sig.add(d)
        cnt = {e: 0 for e in ENGS}
        for o in ops:
            if o.kind != "d" and o.idx in need_sig:
                cnt[o.eng] += 1
                o.sig = cnt[o.eng]
        ndma = 0
        last_on_sem = {}
        NSW = int(_os.environ.get('FW_SWSEM', '0'))
        nsw = 0
        for o in ops:
            if o.kind == "d":
                if NSW and o.eng == "pool":
                    o.dma_sem = NDMA_SEM + nsw
                    o.dma_val = 16
                    o.dma_prev = None
                    nsw += 1
                    assert nsw <= NSW, nsw
                    continue
                s = ndma % NDMA_SEM
                o.dma_sem = s
                o.dma_val = 16 * (ndma // NDMA_SEM + 1)
                o.dma_prev = last_on_sem.get(s)
                last_on_sem[s] = o.idx
                ndma += 1
        self.n_dma = ndma
        sems = {e: self.stack.enter_context(nc.semaphore("sem_" + e)) for e in ENGS}
        dsems = [self.stack.enter_context(nc.semaphore("dsem%d" % i)) for i in range(NDMA_SEM + NSW)]
        final_sem = self.stack.enter_context(nc.semaphore("final"))
        per_eng = {e: [o for o in ops if o.eng == e] for e in ENGS}
        handles = {"pe": nc.tensor, "act": nc.scalar, "dve": nc.vector, "pool": nc.gpsimd, "sp": nc.sync}

        def run(ename, e):
            waited = {}

            def wait(key, sem, val):
                if waited.get(key, 0) >= val:
                    return
                e.wait_ge(sem, val)
                waited[key] = val

            for o in per_eng[ename]:
                for (d, t) in o.deps:
                    p = ops[d]
                    if p.kind == "d":
                        wait(("d", p.dma_sem), dsems[p.dma_sem], p.dma_val)
                    else:
                        if p.sig is None:
                            continue
                        wait(("c", p.eng), sems[p.eng], p.sig)
                if o.kind == "d" and o.dma_prev is not None:
                    p = ops[o.dma_prev]
                    wait(("d", p.dma_sem), dsems[p.dma_sem], p.dma_val)
                ins = o.fn(e)
                if o.kind == "d":
                    ins.then_inc(dsems[o.dma_sem], 16)
                elif o.sig is not None:
                    ins.then_inc(sems[o.eng], 1)
            if ename == "sp":
                fin = {}
                for o in ops:
                    if o.kind == "d":
                        fin[o.dma_sem] = max(fin.get(o.dma_sem, 0), o.dma_val)
                for s, v in fin.items():
                    e.wait_ge(dsems[s], v)

        with nc.Block() as block:
            @block.tensor
            def _(e):
                run("pe", e)

            @block.scalar
            def _(e):
                run("act", e)

            @block.vector
            def _(e):
                run("dve", e)

            @block.gpsimd
            def _(e):
                run("pool", e)

            @block.sync
            def _(e):
                run("sp", e)


def _expand(keys):
    out = []
    for k in keys:
        out.append(k)
    return out

from contextlib import ExitStack
from concourse.bass_utils import run_bass_kernel_spmd

D = 1024
T = 2048
NIN = 6424
OFF = dict(a_q=0, a_k=512, a_v=1024, a_beta=1536, a_alpha=1540, a_gate=1544, b_x=2056, b_B=2568, b_C=2696,
           b_dt=2824, b_gate=2832, c_q=3344, c_k=3856, c_v=4368, c_f=4880, c_gate=4888, d_q=5400, d_gate=5912)
ALPHA = 4.0 ** 0.25
EPS = 1e-6
SLOT = 2064


class St:
    pass


def _r(x):
    return [x] if isinstance(x, V) else []


def _a(x):
    return x.ap if isinstance(x, V) else x


IN_SHAPES = dict(
    xp=[T, D], xs=[16, D], sgc=[2, 4, 3, 1536], sg=[2, 4, 4, 128, 128], ssc=[2, 4, 3, 768], ss=[2, 4, 8, 64, 64],
    fk=[163840, 2048], fv=[163840, 2048], fl=[163840, 32], mk=[2, 4, 256, 512], mv=[2, 4, 256, 512],
    pt=[4, 64], memp=[256, D], win=[2, D, NIN], gcw=[2, 4, 1536], gcb=[2, 1, 1536], gal=[2, 1, 4], gdb=[2, 1, 4],
    gnw=[2, 128, 1], scw=[2, 4, 768], scb=[2, 1, 768], sal=[2, 1, 8], sdb=[2, 1, 8], sd=[2, 1, 8], snw=[2, 512, 1],
    ffb=[2, 1, 8], wmem=[2, D, D], wout=[2, 2048, D], lng=[2, 1, D], lnb=[2, 1, D])
OUT_SHAPES = dict(
    yp=[T, D], ys=[16, D], p_gc=[2, 3, 1536], p_gs=[2, 4, 128, 128], p_sc=[2, 3, 768], p_ss=[2, 8, 64, 64],
    p_fk=[2, T, 512], p_fv=[2, T, 512], p_fl=[2, T, 8], p_mk=[2, 256, 512], p_mv=[2, 256, 512],
    s_gc=[2, 4, 3, 1536], s_gs=[2, 4, 4, 128, 128], s_sc=[2, 4, 3, 768], s_ss=[2, 4, 8, 64, 64],
    s_fk=[2, 16, 512], s_fv=[2, 16, 512], s_fl=[2, 16, 8])


def build(parts=("D", "C", "A", "B"), streams=("p", "s"), nlayers=2, dbg=9, rh=9):
    nc = bass.Bass("TRN2", target_bir_lowering=False)
    small = not ("C" in parts and "s" in streams)
    shp = dict(IN_SHAPES)
    if small:
        shp.update(fk=[128, 2048], fv=[128, 2048], fl=[128, 32])
    I = {k: nc.dram_tensor(k, list(s), I32 if k == "pt" else F32, kind="ExternalInput").ap() for k, s in shp.items()}
    O = {k: nc.dram_tensor(k, list(s), F32, kind="ExternalOutput").ap() for k, s in OUT_SHAPES.items()}
    with ExitStack() as stack:
        P = Prog(nc, stack)
        cnt = [0]

        def DV(ap, key=None):
            cnt[0] += 1
            return V(ap, [key if key is not None else ("dram", cnt[0])])

        def ACTF(out, in_, func, bias=0.0, scale=1.0, accum=None):
            kw = {}
            w = [out]
            if accum is not None:
                kw["accum_out"] = accum.ap
                w.append(accum)
            P.op("act", lambda e: e.activation(out.ap, in_.ap, func, bias=_a(bias), scale=_a(scale), **kw),
                 w=w, r=[in_] + _r(bias) + _r(scale))

        def TT(out, a, b, op, eng="dve"):
            P.op(eng, lambda e: e.tensor_tensor(out.ap, a.ap, b.ap, op), w=[out], r=[a, b])

        def TS(out, a, s1, op0, s2=None, op1=None, eng="dve"):
            if op1 is None:
                fn = lambda e: e.tensor_scalar(out.ap, a.ap, _a(s1), None, op0)
            else:
                fn = lambda e: e.tensor_scalar(out.ap, a.ap, _a(s1), _a(s2), op0, op1)
            P.op(eng, fn, w=[out], r=[a] + _r(s1) + _r(s2))

        def STT(out, a, s, b, op0, op1, eng="dve"):
            P.op(eng, lambda e: e.scalar_tensor_tensor(out.ap, a.ap, _a(s), b.ap, op0, op1), w=[out], r=[a, b] + _r(s))

        def CP(out, in_, eng="dve"):
            if eng == "act":
                P.op("act", lambda e: e.activation(out.ap, in_.ap, AF.Copy), w=[out], r=[in_])
            else:
                P.op(eng, lambda e: e.tensor_copy(out.ap, in_.ap), w=[out], r=[in_])

        def RECIP(out, in_):
            P.op("dve", lambda e: e.reciprocal(out.ap, in_.ap), w=[out], r=[in_])

        def MEMSET(out, val, eng="pool"):
            P.op(eng, lambda e: e.memset(out.ap, val), w=[out])

        def RSUM(out, in_):
            P.op("dve", lambda e: e.reduce_sum(out.ap, in_.ap, AX.X), w=[out], r=[in_])

        def ASEL(out, in_, pattern, op, fill, base, cm):
            P.op("pool", lambda e: e.affine_select(out.ap, in_.ap, pattern, op, fill, base=base, channel_multiplier=cm),
                 w=[out], r=[in_])

        def sub(v, ap):
            return V(ap, v.keys)

        ones = P.sb("ones", [128, 128]); ident = P.sb("ident", [128, 128])
        mU = P.sb("mU", [128, 128]); mUs = P.sb("mUs", [128, 128]); mLs = P.sb("mLs", [128, 128])
        zeros = P.sb("zeros", [128, 128])
        ident_bf = P.sb("ident_bf", [128, 128], BF16); zeros_bf = P.sb("zeros_bf", [128, 512], BF16)
        ones_bf = P.sb("ones_bf", [128, 128], BF16)
        maskneg = P.sb("maskneg", [128, 128], BF16)
        onespad = P.sb("onespad", [128, 2, 128], BF16)
        lastsel = {n: P.sb("lastsel%d" % n, [128, 128]) for n in (128, 4)}
        augKpast = P.sb("augKpast", [3, 128], BF16)
        MEMSET(ones[:], 1.0); MEMSET(zeros[:], 0.0); MEMSET(zeros_bf[:], 0.0); MEMSET(ones_bf[:], 1.0)
        ASEL(ident[:], ones[:], [[-1, 128]], ALU.is_equal, 0.0, 0, 1)
        ASEL(mU[:], ones[:], [[1, 128]], ALU.is_ge, 0.0, 0, -1)
        ASEL(mUs[:], ones[:], [[1, 128]], ALU.is_gt, 0.0, 0, -1)
        ASEL(mLs[:], ones[:], [[-1, 128]], ALU.is_gt, 0.0, 0, 1)
        ASEL(maskneg[:], zeros[:], [[1, 128]], ALU.is_ge, -30000.0, 0, -1)
        CP(ident_bf[:], ident[:], eng="pool")
        for n in (128, 4):
            MEMSET(lastsel[n][:], 0.0)
            ASEL(lastsel[n][0:n, :], ones[0:n, :], [[0, 128]], ALU.is_equal, 0.0, -(n - 1), 1)
        MEMSET(onespad[:], 0.0)
        MEMSET(onespad[:, 0, 0:64], 1.0); MEMSET(onespad[:, 1, 64:128], 1.0)
        MEMSET(augKpast[:], 1.0)

        mixT = P.sb("mixT", [128, 16, T], BF16)
        AR = [P.sb("ar%d" % i, [128, SLOT]) for i in range(5)]
        RB = [P.sb("rb%d" % i, [128, 512]) for i in range(11)]
        XT = P.sb("xtile", [128, D])
        WB = [P.sb("wb%d" % i, [128, 2048], BF16) for i in range(2)]
        PSR = [P.ps("ps%d" % i) for i in range(4)]
        psO = P.ps("psO"); psD = P.ps("psD"); psS = [P.ps("psS0"), P.ps("psS1")]
        cols = P.sb("cols", [128, 16])
        rr = dict(w=0, ps=0, s=0)

        def PS():
            rr["ps"] += 1
            return PSR[rr["ps"] % 4]

        def PSS():
            rr["s"] += 1
            return psS[rr["s"] % 2]

        class WV:
            def __init__(self, view, keys):
                self.view = view; self.keys = keys

            def __getitem__(self, idx):
                return V(self.view[idx], self.keys)

        def Wload(src, a, b, buf=None):
            if buf is None:
                rr["w"] += 1
                buf = WB[rr["w"] % 2]
            view = buf.t[:, 0:a * b].rearrange("p (a b) -> p a b", b=b)
            P.dma(V(view, buf[:].keys), DV(src), eng="pool")
            return WV(view, buf[:].keys)

        def Win(l, c0, w):
            return Wload(I["win"][l, :, c0:c0 + w].rearrange("(k p) c -> p k c", p=128), 8, w)

        def arbf(i):
            return V(AR[i].t[:].bitcast(BF16), AR[i][:].keys)

        def mkst(name, NT, n, ntile, CW, nseq, L, sample):
            s = St(); s.name = name; s.NT = NT; s.n = n; s.ntile = ntile; s.CW = CW; s.nch = NT // CW
            s.nseq = nseq; s.L = L; s.sample = sample; s.tpc = CW // n
            s.xT = P.sb(name + "xT", [128, 8, NT], BF16)
            s.x_in = I["xp"] if not sample else I["xs"]
            s.y_out = O["yp"] if not sample else O["ys"]
            s.pfx = "p_" if not sample else "s_"
            s.nlev = 6 if n == 128 else 1
            return s

        STS = {}
        if "p" in streams:
            STS["p"] = mkst("p", T, 128, 16, 512, 1, T, False)
        if "s" in streams:
            STS["s"] = mkst("s", 16, 4, 4, 16, 4, 4, True)

        def prep_tile(st, t, xt):
            n = st.n
            for half in range(2):
                ps = PS()
                for kk in range(4):
                    k = half * 4 + kk
                    P.tr(ps[:, kk * n:(kk + 1) * n], sub(xt, xt.ap[0:n, k * 128:(k + 1) * 128]), ident[0:n, 0:n])
                CP(st.xT[:, half * 4:(half + 1) * 4, t * n:(t + 1) * n],
                   sub(ps[:], ps.t[:, 0:4 * n].rearrange("p (a b) -> p a b", b=n)), eng="act")

        def proj_fm(st, wv, c0, M, ch, ps):
            CW = st.CW
            for k in range(8):
                P.mm(ps[0:M, 0:CW], wv[:, k, c0:c0 + M], st.xT[:, k, ch * CW:(ch + 1) * CW], start=(k == 0), stop=(k == 7))

        def proj_tm(st, wv, c0, w, t, psv):
            n = st.n
            for k in range(8):
                P.mm(psv, st.xT[:, k, t * n:(t + 1) * n], wv[:, k, c0:c0 + w], start=(k == 0), stop=(k == 7))

        def silu_from(ps_v, tmp_v, out_v):
            ACTF(tmp_v, ps_v, AF.Exp, scale=-1.0)
            TS(tmp_v, tmp_v, 1.0, ALU.add)
            RECIP(tmp_v, tmp_v)
            TT(out_v, ps_v, tmp_v, ALU.mult)

        def zero_init(bank, ncols):
            P.mm(bank[:, 0:ncols], zeros_bf[:, 0:128], zeros_bf[:, 0:ncols], start=True, stop=False)

        def attn_epilogue(st, l, gate_c0, ch, mix_e, ocol0):
            CW = st.CW
            wv = Win(l, gate_c0, 128)
            pg = PS()
            proj_fm(st, wv, 0, 128, ch, pg)
            sg = RB[0][:, 0:CW]; tmp = RB[1][:, 0:CW]; rd = RB[2][:, 0:CW]
            silu_from(pg[:, 0:CW], tmp, sg)
            RECIP(rd, psD[:, ocol0:ocol0 + CW])
            TT(rd, rd, sg, ALU.mult)
            TT(mixT[:, mix_e, ch * CW:(ch + 1) * CW], psO[:, ocol0:ocol0 + CW], rd, ALU.mult)

        memT = P.sb("memT", [128, 8, 256], BF16)
        KTm = P.sb("KTm", [128, 4, 256], BF16)
        Vm = P.sb("Vm", [128, 2, 512], BF16)

        def mem_ctx_from_tok(ktok, vtok):
            for mt in range(2):
                ps = PS()
                for h in range(4):
                    kv = ktok(mt)
                    P.tr(ps[:, h * 128:(h + 1) * 128], sub(kv, kv.ap[:, h * 128:(h + 1) * 128]), ident[:])
                CP(KTm[:, :, mt * 128:(mt + 1) * 128], sub(ps[:], ps.t[:, :].rearrange("p (h m) -> p h m", m=128)), eng="act")
                CP(Vm[:, mt, :], vtok(mt), eng="pool")

        def group_D(st, l):
            n, CW = st.n, st.CW
            QdT = arbf(0)
            if not st.sample:
                kst = [RB[3], RB[4]]; vst = [RB[5], RB[6]]
                for ech in range(4):
                    wv = Wload(I["wmem"][l, :, ech * 256:(ech + 1) * 256].rearrange("(k p) c -> p k c", p=128), 8, 256)
                    for mt in range(2):
                        ps = PS()
                        for k in range(8):
                            P.mm(ps[:, 0:256], memT[:, k, mt * 128:(mt + 1) * 128], wv[:, k, :], start=(k == 0), stop=(k == 7))
                        dst = (kst if ech < 2 else vst)[mt]
                        CP(dst[:, (ech % 2) * 256:(ech % 2 + 1) * 256], ps[:, 0:256], eng="act")
                for mt in range(2):
                    P.dma(DV(O["p_mk"][l, mt * 128:(mt + 1) * 128, :]), kst[mt][:])
                    P.dma(DV(O["p_mv"][l, mt * 128:(mt + 1) * 128, :]), vst[mt][:])
                mem_ctx_from_tok(lambda mt: kst[mt][:], lambda mt: vst[mt][:])
            for h in range(4):
                wv = Win(l, OFF["d_q"] + h * 128, 128)
                for ch in range(st.nch):
                    ps = PS()
                    proj_fm(st, wv, 0, 128, ch, ps)
                    CP(sub(QdT, QdT.ap[:, ch * CW:(ch + 1) * CW]), ps[:, 0:CW], eng="act")
                if not st.sample:
                    ctxs = [(None, ch, ch * CW, CW) for ch in range(st.nch)]
                else:
                    ctxs = [(s, 0, s * 4, 4) for s in range(4)]
                for (s, ch, q0, qw) in ctxs:
                    if s is not None and h == 0:
                        pass
                    if s is not None:
                        for mt in range(2):
                            P.dma(RB[3 + mt][:], DV(I["mk"][l, s, mt * 128:(mt + 1) * 128, :]))
                            P.dma(RB[5 + mt][:], DV(I["mv"][l, s, mt * 128:(mt + 1) * 128, :]))
                        mem_ctx_from_tok(lambda mt: RB[3 + mt][:], lambda mt: RB[5 + mt][:])
                    oc = 0 if s is None else q0
                    if s is None or s == 0:
                        zero_init(psO, CW); zero_init(psD, CW)
                    for mt in range(2):
                        pS = PSS()
                        P.mm(pS[:, 0:qw], KTm[:, h, mt * 128:(mt + 1) * 128], sub(QdT, QdT.ap[:, q0:q0 + qw]))
                        pt_ = V(RB[7].t[:].bitcast(BF16)[:, 0:qw], RB[7][:].keys)
                        ACTF(pt_, pS[:, 0:qw], AF.Exp, scale=128.0 ** -0.5)
                        P.mm(psO[:, oc:oc + qw], Vm[:, mt, h * 128:(h + 1) * 128], pt_, start=False, stop=False, skip_group_check=True)
                        P.mm(psD[:, oc:oc + qw], ones_bf[:, :], pt_, start=False, stop=False, skip_group_check=True)
                    if s is None:
                        attn_epilogue(st, l, OFF["d_gate"] + h * 128, ch, 12 + h, 0)
                if st.sample:
                    attn_epilogue(st, l, OFF["d_gate"] + h * 128, 0, 12 + h, 0)

        def finish_layer(st, l, last):
            n = st.n
            wo = []
            for g2 in range(2):
                for dch in range(2):
                    i = g2 * 2 + dch
                    view = AR[i].t[:].bitcast(BF16)[:, 0:4096].rearrange("p (a b) -> p a b", b=512)
                    P.dma(V(view, AR[i][:].keys),
                          DV(I["wout"][l, g2 * 1024:(g2 + 1) * 1024, dch * 512:(dch + 1) * 512].rearrange("(e p) d -> p e d", p=128)),
                          eng="pool")
                    wo.append(WV(view, AR[i][:].keys))
            lg = AR[4][:, 0:1024]; lb = AR[4][:, 1024:2048]
            P.dma(lg, DV(I["lng"][l, 0:1, :].broadcast_to([128, D])))
            P.dma(lb, DV(I["lnb"][l, 0:1, :].broadcast_to([128, D])))
            junk = V(RB[0].t[:].bitcast(BF16), RB[0][:].keys)
            for t in range(st.ntile):
                rows = slice(t * n, (t + 1) * n)
                xin = XT[0:n, :]
                src = st.x_in[rows, :] if l == 0 else st.y_out[rows, :]
                P.dma(xin, DV(src, (st.name + "y", t)))
                for dch in range(2):
                    ps = PS()
                    for e in range(16):
                        P.mm(ps[0:n, :], mixT[:, e, rows], wo[(e // 8) * 2 + dch][:, e % 8, :], start=(e == 0), stop=(e == 15))
                    STT(sub(xin, XT.t[0:n, dch * 512:(dch + 1) * 512]), sub(xin, XT.t[0:n, dch * 512:(dch + 1) * 512]), ALPHA, ps[0:n, :], ALU.mult, ALU.add)
                c = lambda i: cols[0:n, i:i + 1]
                RSUM(c(0), xin)
                TS(c(1), c(0), -1.0 / D, ALU.mult)
                TS(xin, xin, c(1), ALU.add)
                ACTF(sub(junk, junk.ap[0:n, 0:D]), xin, AF.Square, accum=c(2))
                ACTF(c(3), c(2), AF.Ln, scale=1.0 / D, bias=EPS)
                ACTF(c(3), c(3), AF.Exp, scale=-0.5)
                STT(xin, xin, c(3), sub(lg, AR[4].t[0:n, 0:1024]), ALU.mult, ALU.mult)
                TT(xin, xin, sub(lb, AR[4].t[0:n, 1024:2048]), ALU.add)
                P.dma(DV(st.y_out[rows, :], (st.name + "y", t)), xin)
                if not last:
                    prep_tile(st, t, xin)

        lf_tok = P.sb("lf_tok", [128, 16, 8]); c_tok = P.sb("c_tok", [128, 16, 8]); negc = P.sb("negc", [128, 16, 8])
        cp32 = P.sb("cp32", [128, 16, 8, 3]); cpb = P.sb("cpb", [128, 16, 8], BF16)
        fbias = P.sb("fbias", [128, 8])
        augq = P.sb("augq", [3, 512], BF16)
        ptbuf = [P.sb("ptb%d" % i, [128, 512], BF16) for i in range(2)]
        idx = P.sb("idx", [128, 16], I32); ptb = P.sb("ptbc", [128, 16], I32); pmod = P.sb("pmod", [128, 1])
        Lg = Buf("rb9", RB[9].t[:].rearrange("p (g x) -> p g x", x=32)); Rt = Buf("rb10", RB[10].t[:].rearrange("p (g x) -> p g x", x=32))
        KTp = [P.sb("KTp%d" % i, [128, 4, 128], BF16) for i in range(2)]
        Vpp = [P.sb("Vpp%d" % i, [128, 4, 2, 128], BF16) for i in range(2)]
        for b in Vpp:
            MEMSET(b[:], 0.0)
        for q in range(4):
            P.op("pool", lambda e, q=q: e.iota(pmod.t[32 * q:32 * (q + 1), :], [[0, 1]], base=0, channel_multiplier=1,
                                              allow_small_or_imprecise_dtypes=True), w=[pmod[:]])
        P_kg = [AR[1], AR[2]]
        P_vg = [AR[3], AR[4]]

        def fox_scalars(st, l):
            n, nt = st.n, st.ntile
            wv = Win(l, OFF["c_f"], 8)
            ps = PS()
            for t in range(nt):
                proj_tm(st, wv, 0, 8, t, ps[0:n, t * 8:(t + 1) * 8])
            P.dma(fbias[:], DV(I["ffb"][l, 0:1, :].broadcast_to([128, 8])))
            x = RB[0][0:n, 0:nt * 8]
            x3 = sub(x, RB[0].t[0:n, 0:nt * 8].rearrange("p (t h) -> p t h", h=8))
            TT(x3, sub(ps[:], ps.t[0:n, 0:nt * 8].rearrange("p (t h) -> p t h", h=8)),
               sub(fbias[:], fbias.t[0:n, :].unsqueeze(1).broadcast_to([n, nt, 8])), ALU.add)
            ACTF(x, x, AF.Exp, scale=-1.0)
            ACTF(x, x, AF.Ln, bias=1.0)
            lf = lf_tok[0:n, 0:nt, :]
            TS(lf, x3, -1.0, ALU.mult)
            P.dma(DV(O[st.pfx + "fl"][l].rearrange("(t p) h -> p t h", p=n)), lf)
            ps2 = PS()
            lf2 = sub(lf, lf_tok.t[0:n, 0:nt, :].rearrange("p t h -> p (t h)"))
            P.mm(ps2[0:n, 0:nt * 8], mU[0:n, 0:n], lf2)
            c2 = sub(c_tok[:], c_tok.t[0:n, 0:nt, :].rearrange("p t h -> p (t h)"))
            if st.sample:
                CP(c2, ps2[0:n, 0:nt * 8])
            else:
                ps3 = PS()
                P.mm(ps3[0:n, 0:nt * 8], ones[0:n, 0:n], lf2)
                tot = RB[1][0:n, 0:nt * 8]
                CP(sub(tot, RB[1].t[0:n, 0:nt * 8].rearrange("p (h t) -> p t h", t=nt)),
                   sub(ps3[:], ps3.t[0:n, 0:nt * 8].rearrange("p (t h) -> p t h", h=8)))
                rm = RB[2][0:n, 0:nt * 8]
                MEMSET(rm, 1.0)
                MEMSET(sub(rm, RB[2].t[0:n, 0:nt * 8].rearrange("p (h t) -> p h t", t=nt)[:, :, 0:1]), 0.0)
                inc = RB[3][0:n, 0:nt * 8]
                P.op("dve", lambda e: e.tensor_tensor_scan(inc.ap, rm.ap, tot.ap, 0.0, ALU.mult, ALU.add), w=[inc], r=[rm, tot])
                TT(inc, inc, tot, ALU.subtract)
                TT(sub(c_tok[:], c_tok.t[0:n, 0:nt, :]), sub(ps2[:], ps2.t[0:n, 0:nt * 8].rearrange("p (t h) -> p t h", h=8)),
                   sub(inc, RB[3].t[0:n, 0:nt * 8].rearrange("p (h t) -> p t h", t=nt)), ALU.add)
            TS(sub(negc[:], negc.t[0:n, 0:nt, :].rearrange("p t h -> p (t h)")), c2, -1.0, ALU.mult)
            cb = sub(cpb[:], cpb.t[0:n, 0:nt, :])
            c3 = sub(c_tok[:], c_tok.t[0:n, 0:nt, :])
            res = sub(RB[4][:], RB[4].t[0:n, 0:nt * 8].rearrange("p (t h) -> p t h", h=8))
            CP(res, c3)
            for pc in range(3):
                CP(cb, res)
                pcv = sub(cp32[:], cp32.t[0:n, 0:nt, :, pc])
                CP(pcv, cb)
                if pc < 2:
                    TT(res, res, pcv, ALU.subtract)

        def build_augq(st, h, q0, qw):
            n = st.n
            ps = PS()
            for i in range(qw // n):
                t = q0 // n + i
                P.tr(ps[0:3, i * n:(i + 1) * n], sub(cp32[:], cp32.t[0:n, t, h, :]), ident[0:n, 0:n])
            CP(augq[0:3, 0:qw], ps[0:3, 0:qw], eng="act")

        def ktile_step(nk, heads, qw, KT_of, aug_k, vpad_of, qT_of, bias_v, diag_j0, ocol_of, q_lo=0, prompt_bias=None):
            pS = PSS()
            hb = len(heads)
            for hi, h in enumerate(heads):
                c0 = hi * qw
                P.mm(pS[0:nk, c0 + q_lo:c0 + qw], KT_of(h), qT_of(h, q_lo), start=True, stop=False, skip_group_check=True)
                if diag_j0 is not None:
                    P.mm(pS[0:nk, c0 + q_lo:c0 + q_lo + nk], ident_bf[0:nk, 0:nk], maskneg[0:nk, 0:nk], start=False, stop=False, skip_group_check=True)
                P.mm(pS[0:nk, c0 + q_lo:c0 + qw], aug_k, sub(augq[:], augq.t[0:3, hi * qw + q_lo:hi * qw + qw]) if prompt_bias is not None else aug_q_all(hi, qw, q_lo),
                     start=False, stop=True, skip_group_check=True)
            rr["pt"] = rr.get("pt", 0) + 1
            ptv = ptbuf[rr["pt"] % 2]
            if prompt_bias is not None:
                pt_ = ptv[0:nk, q_lo:qw]
                ACTF(pt_, pS[0:nk, q_lo:qw], AF.Exp, bias=prompt_bias)
            else:
                tmp = RB[8][0:nk, 0:hb * qw]
                TT(sub(tmp, RB[8].t[0:nk, 0:hb * qw].rearrange("p (h q) -> p h q", q=qw)),
                   sub(pS[:], pS.t[0:nk, 0:hb * qw].rearrange("p (h q) -> p h q", q=qw)), bias_v, ALU.add)
                pt_ = ptv[0:nk, 0:hb * qw]
                ACTF(pt_, tmp, AF.Exp)
            for hi, h in enumerate(heads):
                c0 = hi * qw
                oc = ocol_of(h)
                rhs = sub(ptv[:], ptv.t[0:nk, c0 + q_lo:c0 + qw])
                P.mm(psO[:, oc + q_lo:oc + qw], vpad_of(h), rhs, start=False, stop=False, skip_group_check=True)
                P.mm(psD[:, oc + q_lo:oc + qw], onespad[0:nk, h % 2, :], rhs, start=False, stop=False, skip_group_check=True)

        augq_s = P.sb("augq_s", [3, 32], BF16)

        def aug_q_all(hi, qw, q_lo):
            return sub(augq_s[:], augq_s.t[0:3, hi * qw + q_lo:hi * qw + qw])

        def group_C(st, l):
            n, CW, nt = st.n, st.CW, st.ntile
            fox_scalars(st, l)
            if st.sample:
                Vp = [V(RB[4 + pr].t[:].bitcast(BF16), RB[4 + pr][:].keys) for pr in range(4)]
                for pr in range(4):
                    MEMSET(RB[4 + pr][:], 0.0)
            else:
                Vp = [arbf(1 + pr) for pr in range(4)]
                for pr in range(4):
                    MEMSET(AR[1 + pr][:], 0.0)
            for half in range(2):
                wk = Win(l, OFF["c_k"] + half * 256, 256)
                wvv = Win(l, OFF["c_v"] + half * 256, 256)
                for t in range(nt):
                    rows = slice(t * n, (t + 1) * n)
                    ps = PS(); proj_tm(st, wk, 0, 256, t, ps[0:n, 0:256])
                    CP(RB[9][0:n, 0:256], ps[0:n, 0:256], eng="act")
                    P.dma(DV(O[st.pfx + "fk"][l, rows, half * 256:(half + 1) * 256]), RB[9][0:n, 0:256])
                    ps = PS(); proj_tm(st, wvv, 0, 256, t, ps[0:n, 0:256])
                    CP(RB[10][0:n, 0:256], ps[0:n, 0:256], eng="act")
                    P.dma(DV(O[st.pfx + "fv"][l, rows, half * 256:(half + 1) * 256]), RB[10][0:n, 0:256])
                    for pp in range(2):
                        pr = half * 2 + pp
                        dst = Vp[pr].ap[0:n, t * 256:(t + 1) * 256].rearrange("p (b d) -> p b d", d=64)[:, 0::3, :]
                        srcv = RB[10].t[0:n, pp * 128:(pp + 1) * 128].rearrange("p (b d) -> p b d", d=64)
                        CP(V(dst, Vp[pr].keys), V(srcv, RB[10][:].keys), eng="pool")
            QT = arbf(0)
            if st.sample:
                for pr in range(4):
                    for which, off in (("c_q", 0), ("c_k", 64)):
                        wv = Win(l, OFF[which] + pr * 128, 128)
                        ps = PS(); proj_fm(st, wv, 0, 128, 0, ps)
                        if which == "c_q":
                            ACTF(sub(QT, QT.ap[:, off + pr * 16: off + pr * 16 + 16]), ps[:, 0:16], AF.Copy, scale=0.125)
                        else:
                            CP(sub(QT, QT.ap[:, off + pr * 16: off + pr * 16 + 16]), ps[:, 0:16], eng="act")
                zero_init(psO, 64); zero_init(psD, 64)
                for s in range(4):
                    fox_sample_seq(st, l, s, QT, Vp)
                for pr in range(4):
                    attn_epilogue(st, l, OFF["c_gate"] + pr * 128, 0, 8 + pr, pr * 16)
                return
            for pr in range(4):
                for which, off in (("c_q", 0), ("c_k", T)):
                    wv = Win(l, OFF[which] + pr * 128, 128)
                    for ch in range(st.nch):
                        ps = PS(); proj_fm(st, wv, 0, 128, ch, ps)
                        dst = sub(QT, QT.ap[:, off + ch * CW: off + (ch + 1) * CW])
                        if which == "c_q":
                            ACTF(dst, ps[:, 0:CW], AF.Copy, scale=0.125)
                        else:
                            CP(dst, ps[:, 0:CW], eng="act")
                for ch in range(st.nch):
                    zero_init(psO, CW); zero_init(psD, CW)
                    for hp in range(2):
                        h = pr * 2 + hp
                        build_augq(st, h, ch * CW, CW)
                        rws = slice(hp * 64, (hp + 1) * 64)
                        for kt in range(ch * 4 + 4):
                            j = kt - ch * 4
                            q_lo = max(j, 0) * 128
                            ktile_step(
                                128, [h], CW,
                                KT_of=lambda hh: sub(QT, QT.ap[rws, T + kt * 128: T + (kt + 1) * 128]),
                                aug_k=ones_bf[0:3, 0:128],
                                vpad_of=lambda hh: sub(Vp[pr], Vp[pr].ap[:, kt * 256 + hp * 128: kt * 256 + (hp + 1) * 128]),
                                qT_of=lambda hh, ql: sub(QT, QT.ap[rws, ch * CW + ql:(ch + 1) * CW]),
                                bias_v=None, diag_j0=(j if j >= 0 else None), ocol_of=lambda hh: 0, q_lo=q_lo,
                                prompt_bias=negc[:, kt, h:h + 1])
                    attn_epilogue(st, l, OFF["c_gate"] + pr * 128, ch, 8 + pr, 0)

        def fox_sample_seq(st, l, s, QT, Vp):
            for q in range(4):
                P.dma(ptb[32 * q:32 * (q + 1), :], DV(I["pt"][s:s + 1, q::4].broadcast_to([32, 16])), allow_slow_non_contiguous=True)
            TS(idx[:], ptb[:], 32.0, ALU.mult, pmod[:, 0:1], ALU.add)
            for g in range(16):
                P.op("pool", lambda e, g=g: e.indirect_dma_start(
                    out=Lg.t[:, g, :], out_offset=None, in_=I["fl"],
                    in_offset=bass.IndirectOffsetOnAxis(ap=idx.t[:, g:g + 1], axis=0), element_offset=l * 81920 * 32),
                    w=[Lg[:]], r=[idx[:]], kind="d")
            L4 = sub(Lg[:], Lg.t[:].rearrange("p g (r h) -> p g r h", h=8))
            rs = RB[0][:, 0:128]
            P.op("dve", lambda e: e.tensor_reduce(RB[0].t[:, 0:128].rearrange("p (h g) -> p g h", g=16),
                                                  Lg.t[:].rearrange("p g (r h) -> p g h r", h=8), AX.X, ALU.add),
                 w=[rs], r=[Lg[:]])
            rm = RB[1][:, 0:128]
            MEMSET(rm, 1.0)
            MEMSET(sub(rm, RB[1].t[:, 0:128].rearrange("p (h g) -> p h g", g=16)[:, :, 0:1]), 0.0)
            pre = RB[2][:, 0:128]
            P.op("dve", lambda e: e.tensor_tensor_scan(pre.ap, rm.ap, rs.ap, 0.0, ALU.mult, ALU.add), w=[pre], r=[rm, rs])
            rs2 = RB[3][:, 0:128]
            TT(sub(rs2, RB[3].t[:, 0:128].rearrange("p (h g) -> p h g", g=16)),
               sub(pre, RB[2].t[:, 0:128].rearrange("p (h g) -> p h g", g=16)[:, :, 15:16].broadcast_to([128, 8, 16])),
               sub(pre, RB[2].t[:, 0:128].rearrange("p (h g) -> p h g", g=16)), ALU.subtract)
            ps = PS()
            P.mm(ps[:, 0:128], mLs[:, :], rs, start=True, stop=False)
            P.mm(ps[:, 0:128], ones[:, :], rs2, start=False, stop=True)
            R4 = sub(Rt[:], Rt.t[:].rearrange("p g (r h) -> p g r h", h=8))
            MEMSET(Rt[:], 0.0)
            for r in (2, 1, 0):
                TT(sub(Rt[:], R4.ap[:, :, r, :]), sub(Rt[:], R4.ap[:, :, r + 1, :]), sub(Lg[:], L4.ap[:, :, r + 1, :]), ALU.add)
            TT(R4, R4, sub(ps[:], ps.t[:, 0:128].rearrange("p (h g) -> p g h", g=16).unsqueeze(2).broadcast_to([128, 16, 4, 8])), ALU.add)
            psq = PS()
            for h in range(8):
                P.tr(psq[0:3, h * 4:(h + 1) * 4], sub(cp32[:], cp32.t[0:4, s, h, :]), ident[0:4, 0:4])
            CP(augq_s[:], psq[0:3, 0:32], eng="act")
            heads = list(range(8))
            qT_of = lambda h, ql: sub(QT, QT.ap[(h % 2) * 64:(h % 2) * 64 + 64, (h // 2) * 16 + s * 4:(h // 2) * 16 + s * 4 + 4])
            ocol_of = lambda h: (h // 2) * 16 + s * 4
            for g in range(16):
                kg = P_kg[g % 2]; vg = P_vg[g % 2]
                P.op("pool", lambda e, kg=kg, g=g: e.indirect_dma_start(
                    out=kg.t[:, 0:2048], out_offset=None, in_=I["fk"],
                    in_offset=bass.IndirectOffsetOnAxis(ap=idx.t[:, g:g + 1], axis=0), element_offset=l * 81920 * 2048), w=[kg[:]], r=[idx[:]], kind="d")
                P.op("pool", lambda e, vg=vg, g=g: e.indirect_dma_start(
                    out=vg.t[:, 0:2048], out_offset=None, in_=I["fv"],
                    in_offset=bass.IndirectOffsetOnAxis(ap=idx.t[:, g:g + 1], axis=0), element_offset=l * 81920 * 2048), w=[vg[:]], r=[idx[:]], kind="d")
                for r in range(4):
                    kt = g * 4 + r
                    ktp = KTp[kt % 2]; vpp = Vpp[kt % 2]
                    pst = PS()
                    for pr in range(4):
                        P.tr(pst[:, pr * 128:(pr + 1) * 128], sub(kg[:], kg.t[:, r * 512 + pr * 128: r * 512 + (pr + 1) * 128]), ident[:])
                    CP(ktp[:], sub(pst[:], pst.t[:, :].rearrange("p (a b) -> p a b", b=128)), eng="act")
                    CP(V(vpp.t[:].rearrange("p a b (x d) -> p a (b x) d", d=64)[:, :, 0::3, :], vpp[:].keys),
                       sub(vg[:], vg.t[:, r * 512:(r + 1) * 512].rearrange("p (a b d) -> p a b d", b=2, d=64)), eng="pool")
                    ktile_step(
                        128, heads, 4,
                        KT_of=lambda h: sub(ktp[:], ktp.t[(h % 2) * 64:(h % 2) * 64 + 64, h // 2, :]),
                        aug_k=augKpast[0:3, :],
                        vpad_of=lambda h: sub(vpp[:], vpp.t[:, h // 2, h % 2, :]),
                        qT_of=qT_of,
                        bias_v=sub(Rt[:], R4.ap[:, g, r, :].unsqueeze(2).broadcast_to([128, 8, 4])),
                        diag_j0=None, ocol_of=ocol_of)
            ktile_step(
                4, heads, 4,
                KT_of=lambda h: sub(QT, QT.ap[(h % 2) * 64:(h % 2) * 64 + 64, 64 + (h // 2) * 16 + s * 4: 64 + (h // 2) * 16 + s * 4 + 4]),
                aug_k=ones_bf[0:3, 0:4],
                vpad_of=lambda h: sub(Vp[h // 2], Vp[h // 2].ap[0:4, s * 256 + (h % 2) * 128: s * 256 + (h % 2 + 1) * 128]),
                qT_of=qT_of,
                bias_v=sub(negc[:], negc.t[0:4, s, :].unsqueeze(2).broadcast_to([4, 8, 4])),
                diag_j0=0, ocol_of=ocol_of)

        fld = P.sb("fld", [128, 8, 128])
        gl = P.sb("gl", [128, 2, 128])
        hb_ = P.sb("hbc", [128, 4, 8])
        cw = P.sb("cw", [128, 8])
        Sst = P.sb("Sst", [128, 128])
        nwc = P.sb("nwc", [128, 4])
        F_BETA, F_G, F_GAM, F_BE, F_ET, F_NB, F_DT, F_SS = range(8)

        def fv(f, n, nt, H):
            return sub(fld[:], fld.t[0:n, f, 0:nt * H])

        def fv3(f, n, nt, H):
            return sub(fld[:], fld.t[0:n, f, 0:nt * H].rearrange("p (t h) -> p t h", h=H))

        def fcol(f, n, t0, h, H, shape):
            return sub(fld[:], fld.t[0:n, f, :].rearrange("p (t h) -> p t h", h=H)[:, t0:t0 + 4, h:h + 1].broadcast_to(shape))

        def conv_fm(st, l, c0, dst, wname, bname, sname, oname, cch):
            n, CW, L, ns = st.n, st.CW, st.L, st.nseq
            U = sub(AR[0][:], AR[0].t[:, 0:ns * (L + 3)].rearrange("p (s x) -> p s x", x=L + 3))
            if st.sample:
                for sq in range(ns):
                    P.dma(sub(U, U.ap[:, sq, 0:3]), DV(I[sname][l, sq, :, cch:cch + 128].rearrange("r c -> c r")), allow_slow_non_contiguous=True)
            else:
                MEMSET(sub(U, U.ap[:, :, 0:3]), 0.0)
            P.dma(cw[:, 0:4], DV(I[wname][l, :, cch:cch + 128].rearrange("i c -> c i")), allow_slow_non_contiguous=True)
            P.dma(cw[:, 4:5], DV(I[bname][l, 0:1, cch:cch + 128].rearrange("o c -> c o")), allow_slow_non_contiguous=True)
            wv = Win(l, c0, 128)
            for ch in range(st.nch):
                ps = PS(); proj_fm(st, wv, 0, 128, ch, ps)
                if st.sample:
                    CP(sub(U, U.ap[:, :, 3:3 + L]), sub(ps[:], ps.t[:, 0:16].rearrange("p (s x) -> p s x", x=L)), eng="act")
                else:
                    CP(sub(U, U.ap[:, 0, 3 + ch * CW:3 + (ch + 1) * CW]), ps[:, 0:CW], eng="act")
            od = O[st.pfx + oname]
            if st.sample:
                for sq in range(ns):
                    P.dma(DV(od[l, sq, :, cch:cch + 128].rearrange("r c -> c r")), sub(U, U.ap[:, sq, L:L + 3]), allow_slow_non_contiguous=True)
            else:
                P.dma(DV(od[l, :, cch:cch + 128].rearrange("r c -> c r")), sub(U, U.ap[:, 0, L:L + 3]), allow_slow_non_contiguous=True)
            Y = sub(dst, dst.ap[:, 0:ns * L].rearrange("p (s x) -> p s x", x=L))
            TS(Y, sub(U, U.ap[:, :, 3:3 + L]), cw[:, 3:4], ALU.mult, cw[:, 4:5], ALU.add)
            for i in range(3):
                STT(Y, sub(U, U.ap[:, :, i:i + L]), cw[:, i:i + 1], Y, ALU.mult, ALU.add)
            tmp = sub(AR[4][:], AR[4].t[:, 0:ns * L].rearrange("p (s x) -> p s x", x=L))
            silu_from(Y, tmp, Y)

        def tok_scalars(st, l, c0, ncols, H, is_gdn):
            n, nt = st.n, st.ntile
            wv = Win(l, c0, ncols)
            ps = PS()
            for t in range(nt):
                proj_tm(st, wv, 0, ncols, t, ps[0:n, t * ncols:(t + 1) * ncols])
            raw = sub(ps[:], ps.t[0:n, 0:nt * ncols].rearrange("p (t c) -> p t c", c=ncols))
            bc = lambda k: sub(hb_[:], hb_.t[0:n, k, 0:H].unsqueeze(1).broadcast_to([n, nt, H]))
            x = sub(RB[0][:], RB[0].t[0:n, 0:nt * H].rearrange("p (t h) -> p t h", h=H))
            x2 = RB[0][0:n, 0:nt * H]
            if is_gdn:
                P.dma(hb_[:, 0, 0:4], DV(I["gdb"][l, 0:1, :].broadcast_to([128, 4])))
                P.dma(hb_[:, 1, 0:4], DV(I["gal"][l, 0:1, :].broadcast_to([128, 4])))
                ACTF(x, sub(raw, raw.ap[:, :, 0:4]), AF.Exp, scale=-1.0)
                TS(x2, x2, 1.0, ALU.add)
                RECIP(fv(F_BETA, n, nt, H), x2)
                TS(fv(F_NB, n, nt, H), fv(F_BETA, n, nt, H), -1.0, ALU.mult)
                TT(x, sub(raw, raw.ap[:, :, 4:8]), bc(0), ALU.add)
            else:
                P.dma(hb_[:, 0, 0:8], DV(I["sdb"][l, 0:1, :].broadcast_to([128, 8])))
                P.dma(hb_[:, 1, 0:8], DV(I["sal"][l, 0:1, :].broadcast_to([128, 8])))
                P.dma(hb_[:, 2, 0:8], DV(I["sd"][l, 0:1, :].broadcast_to([128, 8])))
                TT(x, raw, bc(0), ALU.add)
            ACTF(x2, x2, AF.Exp)
            ACTF(x2, x2, AF.Ln, bias=1.0)
            ACTF(hb_[:, 1, 0:H], hb_[:, 1, 0:H], AF.Exp)
            TS(hb_[:, 1, 0:H], hb_[:, 1, 0:H], -1.0, ALU.mult)
            if not is_gdn:
                CP(fv(F_DT, n, nt, H), x2)
            TT(fv3(F_G, n, nt, H), x, bc(1), ALU.mult)
            ps2 = PS()
            P.mm(ps2[0:n, 0:nt * H], mU[0:n, 0:n], fv(F_G, n, nt, H))
            CP(fv(F_GAM, n, nt, H), ps2[0:n, 0:nt * H])
            ps3 = PS()
            P.mm(ps3[:, 0:nt * H], lastsel[n][0:n, :], fv(F_GAM, n, nt, H))
            CP(gl[:, 0, 0:nt * H], ps3[:, 0:nt * H])
            ACTF(gl[:, 1, 0:nt * H], ps3[:, 0:nt * H], AF.Exp)
            TT(fv(F_ET, n, nt, H), sub(gl[:], gl.t[0:n, 0, 0:nt * H]), fv(F_GAM, n, nt, H), ALU.subtract)
            ACTF(fv(F_ET, n, nt, H), fv(F_ET, n, nt, H), AF.Exp)
            if is_gdn:
                ACTF(x2, fv(F_GAM, n, nt, H), AF.Exp)
                TT(fv(F_BE, n, nt, H), x2, fv(F_BETA, n, nt, H), ALU.mult)

        def recur_head(st, l, h, H, dk, dv, KT_of, QT_of, VT_of, kbase, vbase, is_gdn, after_o, s_in, s_out):
            n, nt = st.n, st.ntile
            q4 = 4 * n
            kr = slice(kbase, kbase + dk)
            identb = sub(ident[:], ident.t[0:n, 0:n].unsqueeze(1).broadcast_to([n, 4, n]))
            mUb = sub(mU[:], mU.t[0:n, 0:n].unsqueeze(1).broadcast_to([n, 4, n]))
            mLb = sub(mLs[:], mLs.t[0:n, 0:n].unsqueeze(1).broadcast_to([n, 4, n]))
            v3 = lambda rb, w=n: sub(rb[:], rb.t[0:n, 0:4 * w].rearrange("p (q x) -> p q x", x=w))
            v2 = lambda rb, w=n: rb[0:n, 0:4 * w]
            for qd in range(nt // 4):
                t0 = qd * 4
                c0, c1 = t0 * n, (t0 + 4) * n
                TT(v3(RB[0]), identb, fcol(F_GAM, n, t0, h, H, [n, 4, n]), ALU.mult)
                Gps = PS()
                P.mm(Gps[:, 0:q4], ones[0:n, :], v2(RB[0]))
                ACTF(RB[1][:, 0:q4], Gps[:, 0:q4], AF.Exp)
                TT(v3(RB[2]), sub(Gps[:], Gps.t[0:n, 0:q4].rearrange("p (q x) -> p q x", x=n)), fcol(F_GAM, n, t0, h, H, [n, 4, n]), ALU.subtract)
                TS(v2(RB[3]), v2(RB[2]), 0.0, ALU.min)
                ACTF(v2(RB[3]), v2(RB[3]), AF.Exp)
                TT(v3(RB[3]), v3(RB[3]), mUb, ALU.mult)
                psQ = PS()
                for q in range(4):
                    a, b = c0 + q * n, c0 + (q + 1) * n
                    P.mm(psQ[0:n, q * n:(q + 1) * n], KT_of(a, b), QT_of(a, b))
                TT(v2(RB[6]), psQ[0:n, 0:q4], v2(RB[3]), ALU.mult)
                if is_gdn:
                    TS(v2(RB[4]), v2(RB[2]), 0.0, ALU.max)
                    ACTF(v2(RB[4]), v2(RB[4]), AF.Exp, scale=-1.0)
                    TT(v3(RB[4]), v3(RB[4]), mLb, ALU.mult)
                    TT(v3(RB[4]), v3(RB[4]), fcol(F_NB, n, t0, h, H, [n, 4, n]), ALU.mult)
                    psK = PS()
                    for q in range(4):
                        a, b = c0 + q * n, c0 + (q + 1) * n
                        P.mm(psK[0:n, q * n:(q + 1) * n], KT_of(a, b), KT_of(a, b))
                    TT(v2(RB[5]), psK[0:n, 0:q4], v2(RB[4]), ALU.mult)
                    psT = PS()
                    for q in range(4):
                        P.tr(psT[0:n, q * n:(q + 1) * n], RB[5][0:n, q * n:(q + 1) * n], ident[0:n, 0:n])
                    CP(v2(RB[7]), psT[0:n, 0:q4])
                    TT(v3(RB[8]), v3(RB[7]), identb, ALU.add)
                    X, XTb, Xn, XTn = RB[7], RB[5], RB[10], RB[9]
                    for lev in range(1, st.nlev + 1):
                        lastlev = (lev == st.nlev)
                        if not lastlev:
                            psA = PS()
                            for q in range(4):
                                sl = slice(q * n, (q + 1) * n)
                                P.mm(psA[0:n, sl], XTb[0:n, sl], X[0:n, sl])
                        psB = PS()
                        for q in range(4):
                            sl = slice(q * n, (q + 1) * n)
                            P.mm(psB[0:n, sl], X[0:n, sl], XTb[0:n, sl])
                        CP(v2(XTn), psB[0:n, 0:q4], eng="act")
                        if not lastlev:
                            CP(v2(Xn), psA[0:n, 0:q4])
                        psC = PS()
                        for q in range(4):
                            sl = slice(q * n, (q + 1) * n)
                            P.mm(psC[0:n, sl], XTn[0:n, sl], RB[8][0:n, sl])
                        TT(v2(RB[8]), v2(RB[8]), psC[0:n, 0:q4], ALU.add)
                        X, XTb, Xn, XTn = Xn, XTn, X, XTb
                if rh < 1:
                    continue
                psKt = PS()
                for q in range(4):
                    a, b = c0 + q * n, c0 + (q + 1) * n
                    P.tr(psKt[0:n, q * dk:(q + 1) * dk], KT_of(a, b), ident[kr, kr])
                psVt = PS()
                vr = slice(vbase, vbase + dv)
                for q in range(4):
                    a, b = c0 + q * n, c0 + (q + 1) * n
                    P.tr(psVt[0:n, q * dv:(q + 1) * dv], VT_of(a, b), ident[vr, vr])
                Kt3 = sub(psKt[:], psKt.t[0:n, 0:4 * dk].rearrange("p (q x) -> p q x", x=dk))
                Vt3 = sub(psVt[:], psVt.t[0:n, 0:4 * dv].rearrange("p (q x) -> p q x", x=dv))
                ktl = sub(RB[4][:], RB[4].t[0:n, 0:512].rearrange("p (q x) -> p q x", x=128))
                if dk < 128:
                    MEMSET(RB[4][:], 0.0)
                TT(sub(ktl, ktl.ap[:, :, kbase:kbase + dk]), Kt3, fcol(F_ET, n, t0, h, H, [n, 4, dk]), ALU.mult)
                if is_gdn:
                    TT(v3(RB[0], 128), Kt3, fcol(F_BE, n, t0, h, H, [n, 4, dk]), ALU.mult)
                    TT(v3(RB[2], 128), Vt3, fcol(F_BETA, n, t0, h, H, [n, 4, dv]), ALU.mult)
                    psW = PS()
                    for q in range(4):
                        P.mm(psW[:, q * n:(q + 1) * n], RB[0][0:n, q * 128:(q + 1) * 128], RB[8][0:n, q * n:(q + 1) * n])
                    CP(RB[5][:, 0:q4], psW[:, 0:q4], eng="act")
                    psV2 = PS()
                    for q in range(4):
                        P.mm(psV2[0:n, q * 128:(q + 1) * 128], RB[8][0:n, q * n:(q + 1) * n], RB[2][0:n, q * 128:(q + 1) * 128])
                    CP(RB[7][0:n, :], psV2[0:n, :])
                else:
                    CP(v3(RB[0], dv), Vt3, eng="act")
                    TT(v3(RB[2], dv), Vt3, fcol(F_DT, n, t0, h, H, [n, 4, dv]), ALU.mult)
                if rh < 2:
                    continue
                qd_ = sub(RB[9][:], RB[9].t[kr, 0:q4])
                TT(qd_, QT_of(c0, c1), sub(RB[1][:], RB[1].t[kr, 0:q4]), ALU.mult)
                if rh < 3:
                    continue
                for q in range(4):
                    t = t0 + q
                    sl = slice(q * n, (q + 1) * n)
                    S = Sst[kr, 0:dv]
                    if (st.sample or t == 0) and rh >= 4:
                        s_in(S, t)
                    if is_gdn:
                        psu = PS()
                        P.mm(psu[0:n, 0:dv], RB[5][:, sl], S)
                        u = RB[1][0:n, 0:dv]
                        TT(u, RB[7][0:n, q * 128:(q + 1) * 128], psu[0:n, 0:dv], ALU.subtract)
                    else:
                        u = RB[2][0:n, q * dv:(q + 1) * dv]
                    pso = PS()
                    P.mm(pso[0:n, 0:dv], sub(RB[9][:], RB[9].t[kr, sl]), S, start=True, stop=False)
                    P.mm(pso[0:n, 0:dv], RB[6][0:n, sl], u, start=False, stop=True)
                    pss = PS()
                    P.mm(pss[:, 0:dv], RB[4][0:n, q * 128:(q + 1) * 128], u)
                    STT(S, S, gl[kr, 1, t * H + h:t * H + h + 1], pss[kr, 0:dv], ALU.mult, ALU.add)
                    after_o(pso, t, q)
                    if (st.sample or t == nt - 1) and rh >= 5:
                        s_out(S, t)

        def group_A(st, l):
            n, nt, CW = st.n, st.ntile, st.CW
            tok_scalars(st, l, OFF["a_beta"], 8, 4, True)
            P.dma(nwc[:, 0:1], DV(I["gnw"][l, :, :]))
            for h in range(4):
                conv_fm(st, l, OFF["a_q"] + h * 128, AR[1][:], "gcw", "gcb", "sgc", "gc", h * 128)
                conv_fm(st, l, OFF["a_k"] + h * 128, AR[2][:], "gcw", "gcb", "sgc", "gc", 512 + h * 128)
                conv_fm(st, l, OFF["a_v"] + h * 128, AR[3][:], "gcw", "gcb", "sgc", "gc", 1024 + h * 128)
                for X, lbias in ((AR[1], -0.5 * float(np.log(128.0))), (AR[2], 0.0)):
                    ACTF(AR[4][:, 0:st.NT], X[:, 0:st.NT], AF.Square)
                    for ch in range(st.nch):
                        cs = slice(ch * CW, (ch + 1) * CW)
                        ps = PS()
                        P.mm(ps[:, 0:CW], ones[:, :], AR[4][:, cs])
                        ACTF(RB[0][:, 0:CW], ps[:, 0:CW], AF.Ln, bias=EPS)
                        ACTF(RB[0][:, 0:CW], RB[0][:, 0:CW], AF.Exp, scale=-0.5, bias=lbias)
                        TT(X[:, cs], X[:, cs], RB[0][:, 0:CW], ALU.mult)

                def s_in(S, t, h=h):
                    if st.sample:
                        P.dma(S, DV(I["sg"][l, t, h]))
                    else:
                        MEMSET(S, 0.0)

                def s_out(S, t, h=h):
                    if st.sample:
                        P.dma(DV(O["s_gs"][l, t, h]), S)
                    else:
                        P.dma(DV(O["p_gs"][l, h]), S)

                def after_o(pso, t, q, h=h):
                    c = lambda i: cols[0:n, i:i + 1]
                    ACTF(RB[3][0:n, 0:128], pso[0:n, 0:128], AF.Square, accum=c(4))
                    ACTF(c(5), c(4), AF.Ln, scale=1.0 / 128, bias=EPS)
                    ACTF(c(5), c(5), AF.Exp, scale=-0.5)
                    TS(RB[3][0:n, 128:256], pso[0:n, 0:128], c(5), ALU.mult)
                    pT = PS()
                    P.tr(pT[:, 0:n], RB[3][0:n, 128:256], ident[0:n, 0:n])
                    ACTF(mixT[:, h, t * n:(t + 1) * n], pT[:, 0:n], AF.Copy, scale=nwc[:, 0:1])

                recur_head(st, l, h, 4, 128, 128,
                           lambda a, b: AR[2][:, a:b], lambda a, b: AR[1][:, a:b], lambda a, b: AR[3][:, a:b],
                           0, 0, True, after_o, s_in, s_out)
                wv = Win(l, OFF["a_gate"] + h * 128, 128)
                for ch in range(st.nch):
                    cs = slice(ch * CW, (ch + 1) * CW)
                    pg = PS(); proj_fm(st, wv, 0, 128, ch, pg)
                    silu_from(pg[:, 0:CW], RB[1][:, 0:CW], RB[0][:, 0:CW])
                    TT(mixT[:, h, cs], mixT[:, h, cs], RB[0][:, 0:CW], ALU.mult)

        def group_B(st, l):
            n, nt, CW = st.n, st.ntile, st.CW
            tok_scalars(st, l, OFF["b_dt"], 8, 8, False)
            MEMSET(fv(F_SS, n, nt, 8), 0.0)
            conv_fm(st, l, OFF["b_B"], AR[1][:], "scw", "scb", "ssc", "sc", 512)
            conv_fm(st, l, OFF["b_C"], AR[2][:], "scw", "scb", "ssc", "sc", 640)
            for pr in range(4):
                conv_fm(st, l, OFF["b_x"] + pr * 128, AR[3][:], "scw", "scb", "ssc", "sc", pr * 128)
                for hp in range(2):
                    h = pr * 2 + hp
                    g = h // 4
                    gr = slice(g * 64, (g + 1) * 64)
                    xr = slice(hp * 64, (hp + 1) * 64)
                    wg = Win(l, OFF["b_gate"] + h * 64, 64)

                    def s_in(S, t, h=h, gr=gr):
                        if st.sample:
                            P.dma(RB[10][0:64, 0:64], DV(I["ss"][l, t, h]))
                            pT = PS()
                            P.tr(pT[0:64, 0:64], RB[10][0:64, 0:64], ident[0:64, 0:64])
                            CP(S, pT[0:64, 0:64], eng="act")
                        else:
                            MEMSET(S, 0.0)

                    def s_out(S, t, h=h, gr=gr):
                        dst = O["s_ss"][l, t, h] if st.sample else O["p_ss"][l, h]
                        pT = PS()
                        P.tr(pT[0:64, 0:64], S, ident[gr, gr])
                        CP(RB[10][0:64, 64:128], pT[0:64, 0:64], eng="act")
                        P.dma(DV(dst), RB[10][0:64, 64:128])

                    def after_o(pso, t, q, h=h, hp=hp, pr=pr, wg=wg):
                        y = RB[3][0:n, 0:64]
                        STT(y, RB[0][0:n, q * 64:(q + 1) * 64], hb_[0:n, 2, h:h + 1], pso[0:n, 0:64], ALU.mult, ALU.add)
                        psg = PS()
                        proj_tm(st, wg, 0, 64, t, psg[0:n, 0:64])
                        silu_from(psg[0:n, 0:64], RB[3][0:n, 64:128], RB[3][0:n, 128:192])
                        z = RB[3][0:n, 192:256]
                        TT(z, y, RB[3][0:n, 128:192], ALU.mult)
                        ACTF(RB[3][0:n, 256:320], z, AF.Square, accum=sub(fld[:], fld.t[0:n, F_SS, t * 8 + h:t * 8 + h + 1]))
                        pT = PS()
                        P.tr(pT[0:64, 0:n], z, ident[0:n, 0:n])
                        CP(mixT[hp * 64:(hp + 1) * 64, 4 + pr, t * n:(t + 1) * n], pT[0:64, 0:n], eng="act")

                    if dbg >= 2:
                        recur_head(st, l, h, 8, 64, 64,
                                   lambda a, b, gr=gr: AR[1][gr, a:b], lambda a, b, gr=gr: AR[2][gr, a:b],
                                   lambda a, b, xr=xr: AR[3][xr, a:b],
                                   g * 64, hp * 64, False, after_o if dbg >= 3 else (lambda *a: None), s_in, s_out)
            if dbg < 4:
                return
            ssv = sub(fld[:], fld.t[0:n, F_SS, 0:nt * 8].rearrange("p (t h) -> p t h", h=8))
            RSUM(cols[0:n, 0:nt], ssv)
            ACTF(cols[0:n, 0:nt], cols[0:n, 0:nt], AF.Ln, scale=1.0 / 512, bias=EPS)
            ACTF(cols[0:n, 0:nt], cols[0:n, 0:nt], AF.Exp, scale=-0.5)
            for e in range(4):
                P.dma(nwc[:, e:e + 1], DV(I["snw"][l, e * 128:(e + 1) * 128, :]))
            for ch in range(st.nch):
                t0 = ch * 4
                d0 = sub(RB[0][:], RB[0].t[0:n, 0:4 * n].rearrange("p (q x) -> p q x", x=n))
                TT(d0, sub(ident[:], ident.t[0:n, 0:n].unsqueeze(1).broadcast_to([n, 4, n])),
                   sub(cols[:], cols.t[0:n, t0:t0 + 4].unsqueeze(2).broadcast_to([n, 4, n])), ALU.mult)
                Rps = PS()
                P.mm(Rps[:, 0:4 * n], ones[0:n, :], RB[0][0:n, 0:4 * n])
                cs = slice(ch * CW, (ch + 1) * CW)
                for e in range(4):
                    STT(mixT[:, 4 + e, cs], mixT[:, 4 + e, cs], nwc[:, e:e + 1], Rps[:, 0:4 * n], ALU.mult, ALU.mult)

        for st in STS.values():
            for t in range(st.ntile):
                xin = XT[0:st.n, :]
                P.dma(xin, DV(st.x_in[t * st.n:(t + 1) * st.n, :]))
                prep_tile(st, t, xin)
        if "p" in STS:
            for mt in range(2):
                P.dma(XT[:, :], DV(I["memp"][mt * 128:(mt + 1) * 128, :]))
                for half in range(2):
                    ps = PS()
                    for kk in range(4):
                        k = half * 4 + kk
                        P.tr(ps[:, kk * 128:(kk + 1) * 128], XT[:, k * 128:(k + 1) * 128], ident[:])
                    CP(memT[:, half * 4:(half + 1) * 4, mt * 128:(mt + 1) * 128],
                       sub(ps[:], ps.t[:, :].rearrange("p (a b) -> p a b", b=128)), eng="act")
        for l in range(nlayers):
            for st in STS.values():
                MEMSET(mixT[:, :, 0:st.NT], 0.0) if l == 0 else None
                if "D" in parts:
                    group_D(st, l)
                if "C" in parts:
                    group_C(st, l)
                if "A" in parts:
                    group_A(st, l)
                if "B" in parts:
                    group_B(st, l)
                finish_layer(st, l, last=(l == nlayers - 1))
        P.emit()
    nc._small_cache = small
    return nc


_NC = {}


def kernel(**inputs):
    key = "full"
    if key not in _NC:
        _NC[key] = build()
    nc = _NC[key]
    f = lambda a: np.ascontiguousarray(a)
    g = inputs
    fk = g["cache_fox_k"].reshape(163840, 2048)
    fv = g["cache_fox_v"].reshape(163840, 2048)
    fl = g["cache_fox_logf"].reshape(163840, 32)
    if getattr(nc, '_small_cache', False):
        fk, fv, fl = fk[:128], fv[:128], fl[:128]
    in_maps = []
    for c in range(8):
        sl = slice(4 * c, 4 * c + 4)
        m = dict(
            xp=f(g["x_prompt"][c]), xs=f(g["x_sample"][sl].reshape(16, D)),
            sgc=f(g["state_gdn_conv"][:, sl]), sg=f(g["state_gdn"][:, sl]),
            ssc=f(g["state_ssd_conv"][:, sl]), ss=f(g["state_ssd"][:, sl]),
            fk=fk, fv=fv, fl=fl,
            mk=f(g["cache_mem_k"][:, sl].reshape(2, 4, 256, 512)), mv=f(g["cache_mem_v"][:, sl].reshape(2, 4, 256, 512)),
            pt=f(g["page_table"][sl]), memp=f(g["mem_prompt"][c]), win=g["w_in"],
            gcw=g["gdn_conv_w"], gcb=g["gdn_conv_b"].reshape(2, 1, 1536), gal=g["gdn_a_log"].reshape(2, 1, 4),
            gdb=g["gdn_dt_bias"].reshape(2, 1, 4), gnw=g["gdn_norm_w"].reshape(2, 128, 1),
            scw=g["ssd_conv_w"], scb=g["ssd_conv_b"].reshape(2, 1, 768), sal=g["ssd_a_log"].reshape(2, 1, 8),
            sdb=g["ssd_dt_bias"].reshape(2, 1, 8), sd=g["ssd_d"].reshape(2, 1, 8), snw=g["ssd_norm_w"].reshape(2, 512, 1),
            ffb=g["fox_f_bias"].reshape(2, 1, 8), wmem=g["w_mem_kv"], wout=g["w_out"],
            lng=g["ln_g"].reshape(2, 1, D), lnb=g["ln_b"].reshape(2, 1, D))
        in_maps.append({k: f(np.asarray(v)) for k, v in m.items()})
    res = run_bass_kernel_spmd(nc, in_maps, core_ids=list(range(8)))
    R = res.results
    cat = lambda k, ax: np.concatenate([np.expand_dims(r[k], ax) if False else r[k] for r in R], axis=ax)
    y_p = np.stack([r["yp"] for r in R], 0)
    y_s = np.stack([r["ys"].reshape(4, 4, D) for r in R], 0).reshape(32, 4, D)
    pstk = lambda k, shp: np.stack([r[k] for r in R], 1).reshape(shp)
    scat = lambda k, shp: np.concatenate([r[k] for r in R], 1).reshape(shp)
    return (y_p, y_s,
            pstk("p_gc", (2, 8, 3, 1536)), pstk("p_gs", (2, 8, 4, 128, 128)), pstk("p_sc", (2, 8, 3, 768)),
            pstk("p_ss", (2, 8, 8, 64, 64)), pstk("p_fk", (2, 8, T, 8, 64)), pstk("p_fv", (2, 8, T, 8, 64)),
            pstk("p_fl", (2, 8, T, 8)), pstk("p_mk", (2, 8, 256, 4, 128)), pstk("p_mv", (2, 8, 256, 4, 128)),
            scat("s_gc", (2, 32, 3, 1536)), scat("s_gs", (2, 32, 4, 128, 128)), scat("s_sc", (2, 32, 3, 768)),
            scat("s_ss", (2, 32, 8, 64, 64)), scat("s_fk", (2, 32, 4, 8, 64)), scat("s_fv", (2, 32, 4, 8, 64)),
            scat("s_fl", (2, 32, 4, 8)))
```

```python
import numpy as np
import concourse.bass as bass
import concourse.mybir as mybir

F32 = mybir.dt.float32
BF16 = mybir.dt.bfloat16
I32 = mybir.dt.int32
ALU = mybir.AluOpType
AF = mybir.ActivationFunctionType
AX = mybir.AxisListType

ENGS = ("pe", "act", "dve", "pool", "sp")
import os as _os
NDMA_SEM = int(_os.environ.get('FW_NDMA', '24'))


class V:
    __slots__ = ("ap", "keys")

    def __init__(self, ap, keys):
        self.ap = ap
        self.keys = tuple(keys)


class Buf:
    def __init__(self, name, t, nsub=0):
        self.name = name
        self.t = t

    def __getitem__(self, idx):
        return V(self.t[idx], ((self.name,),))

    def k(self, *sub):
        return _Sub(self, sub)


class _Sub:
    def __init__(self, buf, sub):
        self.buf = buf
        self.sub = sub

    def __getitem__(self, idx):
        return V(self.buf.t[idx], ((self.buf.name,) + tuple(self.sub),))


class Op:
    __slots__ = ("eng", "fn", "reads", "writes", "kind", "deps", "sig", "dma_sem", "dma_val", "idx", "dma_prev")

    def __init__(self, eng, fn, reads, writes, kind):
        self.eng = eng
        self.fn = fn
        self.reads = reads
        self.writes = writes
        self.kind = kind
        self.deps = []
        self.sig = None
        self.dma_sem = None
        self.dma_val = None
        self.dma_prev = None


class Prog:
    def __init__(self, nc, stack):
        self.nc = nc
        self.stack = stack
        self.ops = []
        self.last_w = {}
        self.readers = {}
        self.nbuf = 0
        self.psum_rr = 0

    def sb(self, name, shape, dt=F32):
        t = self.stack.enter_context(self.nc.sbuf_tensor(name, list(shape), dt))
        return Buf(name, t)

    def ps(self, name, shape=(128, 512), dt=F32):
        t = self.stack.enter_context(self.nc.psum_tensor(name, list(shape), dt))
        return Buf(name, t)

    def op(self, eng, fn, w=(), r=(), kind="c"):
        reads = []
        for v in r:
            reads.extend(v.keys)
        writes = []
        for v in w:
            writes.extend(v.keys)
        writes = writes + [k for k in reads if k[0].startswith("ps") and k not in writes]
        o = Op(eng, fn, reads, writes, kind)
        o.idx = len(self.ops)
        deps = set()
        for k in _expand(reads):
            lw = self.last_w.get(k)
            if lw is not None:
                deps.add((lw, "raw"))
        for k in _expand(writes):
            lw = self.last_w.get(k)
            if lw is not None:
                deps.add((lw, "waw"))
            for rd in self.readers.get(k, ()):
                deps.add((rd, "war"))
        for k in reads:
            self.readers.setdefault(k, []).append(o.idx)
        for k in writes:
            self.last_w[k] = o.idx
            self.readers[k] = []
        o.deps = [(d, t) for (d, t) in deps if d != o.idx]
        self.ops.append(o)
        return o

    def mm(self, out, lhsT, rhs, start=True, stop=True, extra_r=(), **kw):
        return self.op("pe", lambda e: e.matmul(out.ap, lhsT.ap, rhs.ap, start=start, stop=stop, **kw),
                       w=[out], r=[lhsT, rhs] + list(extra_r), kind="mm")

    def tr(self, out, in_, ident):
        return self.op("pe", lambda e: e.transpose(out.ap, in_.ap, ident.ap), w=[out], r=[in_, ident], kind="mm")

    def dma(self, out, in_, eng="sp", **kw):
        return self.op(eng, lambda e: e.dma_start(out=out.ap, in_=in_.ap, **kw), w=[out], r=[in_], kind="d")

    def emit(self):
        nc = self.nc
        ops = self.ops
        need_sig = set()
        for o in ops:
            for (d, t) in o.deps:
                p = ops[d]
                if p.kind == "d":
                    continue
                if p.eng == o.eng and o.kind != "d":
                    if p.eng == "pe":
                        continue
                need_sig.add(d)
        cnt = {e: 0 for e in ENGS}
        for o in ops:
            if o.kind != "d" and o.idx in need_sig:
                cnt[o.eng] += 1
                o.sig = cnt[o.eng]
        ndma = 0
        last_on_sem = {}
        NSW = int(_os.environ.get('FW_SWSEM', '0'))
        nsw = 0
        for o in ops:
            if o.kind == "d":
                if NSW and o.eng == "pool":
                    o.dma_sem = NDMA_SEM + nsw
                    o.dma_val = 16
                    o.dma_prev = None
                    nsw += 1
                    assert nsw <= NSW, nsw
                    continue
                s = ndma % NDMA_SEM
                o.dma_sem = s
                o.dma_val = 16 * (ndma // NDMA_SEM + 1)
                o.dma_prev = last_on_sem.get(s)
                last_on_sem[s] = o.idx
                ndma += 1
        self.n_dma = ndma
        sems = {e: self.stack.enter_context(nc.semaphore("sem_" + e)) for e in ENGS}
        dsems = [self.stack.enter_context(nc.semaphore("dsem%d" % i)) for i in range(NDMA_SEM + NSW)]
        final_sem = self.stack.enter_context(nc.semaphore("final"))
        per_eng = {e: [o for o in ops if o.eng == e] for e in ENGS}
        handles = {"pe": nc.tensor, "act": nc.scalar, "dve": nc.vector, "pool": nc.gpsimd, "sp": nc.sync}

        def run(ename, e):
            waited = {}

            def wait(key, sem, val):
                if waited.get(key, 0) >= val:
                    return
                e.wait_ge(sem, val)
                waited[key] = val

            for o in per_eng[ename]:
                need = {}
                for (d, t) in o.deps:
                    p = ops[d]
                    if p.kind == "d":
                        k = ("d", p.dma_sem); v = p.dma_val; sm = dsems[p.dma_sem]
                    else:
                        if p.sig is None:
                            continue
                        if p.eng == ename and ename == "pe":
                            continue
                        k = ("c", p.eng); v = p.sig; sm = sems[p.eng]
                    if k not in need or need[k][1] < v:
                        need[k] = (sm, v)
                if o.kind == "d" and o.dma_prev is not None:
                    p = ops[o.dma_prev]
                    k = ("d", p.dma_sem)
                    if k not in need or need[k][1] < p.dma_val:
                        need[k] = (dsems[p.dma_sem], p.dma_val)
                for k, (sm, v) in need.items():
                    wait(k, sm, v)
                ins = o.fn(e)
                if o.kind == "d":
                    ins.then_inc(dsems[o.dma_sem], 16)
                elif o.sig is not None:
                    ins.then_inc(sems[o.eng], 1)
            if ename == "sp":
                fin = {}
                for o in ops:
                    if o.kind == "d":
                        fin[o.dma_sem] = max(fin.get(o.dma_sem, 0), o.dma_val)
                for s, v in fin.items():
                    e.wait_ge(dsems[s], v)

        with nc.Block() as block:
            @block.tensor
            def _(e):
                run("pe", e)

            @block.scalar
            def _(e):
                run("act", e)

            @block.vector
            def _(e):
                run("dve", e)

            @block.gpsimd
            def _(e):
                run("pool", e)

            @block.sync
            def _(e):
                run("sp", e)


def _expand(keys):
    out = []
    for k in keys:
        out.append(k)
    return out

from contextlib import ExitStack
from concourse.bass_utils import run_bass_kernel_spmd

D = 1024
T = 2048
NIN = 6424
OFF = dict(a_q=0, a_k=512, a_v=1024, a_beta=1536, a_alpha=1540, a_gate=1544, b_x=2056, b_B=2568, b_C=2696,
           b_dt=2824, b_gate=2832, c_q=3344, c_k=3856, c_v=4368, c_f=4880, c_gate=4888, d_q=5400, d_gate=5912)
ALPHA = 4.0 ** 0.25
EPS = 1e-6
SLOT = 2064


class St:
    pass


def _r(x):
    return [x] if isinstance(x, V) else []


def _a(x):
    return x.ap if isinstance(x, V) else x


IN_SHAPES = dict(
    xp=[T, D], xs=[16, D], sgc=[2, 4, 3, 1536], sg=[2, 4, 4, 128, 128], ssc=[2, 4, 3, 768], ss=[2, 4, 8, 64, 64],
    fk=[163840, 2048], fv=[163840, 2048], fl=[163840, 32], mk=[2, 4, 256, 512], mv=[2, 4, 256, 512],
    pt=[4, 64], memp=[256, D], win=[2, D, NIN], gcw=[2, 4, 1536], gcb=[2, 1, 1536], gal=[2, 1, 4], gdb=[2, 1, 4],
    gnw=[2, 128, 1], scw=[2, 4, 768], scb=[2, 1, 768], sal=[2, 1, 8], sdb=[2, 1, 8], sd=[2, 1, 8], snw=[2, 512, 1],
    ffb=[2, 1, 8], wmem=[2, D, D], wout=[2, 2048, D], lng=[2, 1, D], lnb=[2, 1, D])
OUT_SHAPES = dict(
    yp=[T, D], ys=[16, D], p_gc=[2, 3, 1536], p_gs=[2, 4, 128, 128], p_sc=[2, 3, 768], p_ss=[2, 8, 64, 64],
    p_fk=[2, T, 512], p_fv=[2, T, 512], p_fl=[2, T, 8], p_mk=[2, 256, 512], p_mv=[2, 256, 512],
    s_gc=[2, 4, 3, 1536], s_gs=[2, 4, 4, 128, 128], s_sc=[2, 4, 3, 768], s_ss=[2, 4, 8, 64, 64],
    s_fk=[2, 16, 512], s_fv=[2, 16, 512], s_fl=[2, 16, 8])


def build(parts=("D", "C", "A", "B"), streams=("p", "s"), nlayers=2, dbg=9, rh=9):
    nc = bass.Bass("TRN2", target_bir_lowering=False)
    small = not ("C" in parts and "s" in streams)
    shp = dict(IN_SHAPES)
    if small:
        shp.update(fk=[128, 2048], fv=[128, 2048], fl=[128, 32])
    I = {k: nc.dram_tensor(k, list(s), I32 if k == "pt" else F32, kind="ExternalInput").ap() for k, s in shp.items()}
    O = {k: nc.dram_tensor(k, list(s), F32, kind="ExternalOutput").ap() for k, s in OUT_SHAPES.items()}
    with ExitStack() as stack:
        P = Prog(nc, stack)
        cnt = [0]

        def DV(ap, key=None):
            cnt[0] += 1
            return V(ap, [key if key is not None else ("dram", cnt[0])])

        def ACTF(out, in_, func, bias=0.0, scale=1.0, accum=None):
            kw = {}
            w = [out]
            if accum is not None:
                kw["accum_out"] = accum.ap
                w.append(accum)
            P.op("act", lambda e: e.activation(out.ap, in_.ap, func, bias=_a(bias), scale=_a(scale), **kw),
                 w=w, r=[in_] + _r(bias) + _r(scale))

        def TT(out, a, b, op, eng="dve"):
            P.op(eng, lambda e: e.tensor_tensor(out.ap, a.ap, b.ap, op), w=[out], r=[a, b])

        def TS(out, a, s1, op0, s2=None, op1=None, eng="dve"):
            if op1 is None:
                fn = lambda e: e.tensor_scalar(out.ap, a.ap, _a(s1), None, op0)
            else:
                fn = lambda e: e.tensor_scalar(out.ap, a.ap, _a(s1), _a(s2), op0, op1)
            P.op(eng, fn, w=[out], r=[a] + _r(s1) + _r(s2))

        def STT(out, a, s, b, op0, op1, eng="dve"):
            P.op(eng, lambda e: e.scalar_tensor_tensor(out.ap, a.ap, _a(s), b.ap, op0, op1), w=[out], r=[a, b] + _r(s))

        def CP(out, in_, eng="dve"):
            if eng == "act":
                P.op("act", lambda e: e.activation(out.ap, in_.ap, AF.Copy), w=[out], r=[in_])
            else:
                P.op(eng, lambda e: e.tensor_copy(out.ap, in_.ap), w=[out], r=[in_])

        def RECIP(out, in_):
            P.op("dve", lambda e: e.reciprocal(out.ap, in_.ap), w=[out], r=[in_])

        def MEMSET(out, val, eng="pool"):
            P.op(eng, lambda e: e.memset(out.ap, val), w=[out])

        def RSUM(out, in_):
            P.op("dve", lambda e: e.reduce_sum(out.ap, in_.ap, AX.X), w=[out], r=[in_])

        def ASEL(out, in_, pattern, op, fill, base, cm):
            P.op("pool", lambda e: e.affine_select(out.ap, in_.ap, pattern, op, fill, base=base, channel_multiplier=cm),
                 w=[out], r=[in_])

        def sub(v, ap):
            return V(ap, v.keys)

        ones = P.sb("ones", [128, 128]); ident = P.sb("ident", [128, 128])
        mU = P.sb("mU", [128, 128]); mUs = P.sb("mUs", [128, 128]); mLs = P.sb("mLs", [128, 128])
        zeros = P.sb("zeros", [128, 128])
        ident_bf = P.sb("ident_bf", [128, 128], BF16); zeros_bf = P.sb("zeros_bf", [128, 512], BF16)
        ones_bf = P.sb("ones_bf", [128, 128], BF16)
        maskneg = P.sb("maskneg", [128, 128], BF16)
        onespad = P.sb("onespad", [128, 2, 128], BF16)
        lastsel = {n: P.sb("lastsel%d" % n, [128, 128]) for n in (128, 4)}
        augKpast = P.sb("augKpast", [3, 128], BF16)
        MEMSET(ones[:], 1.0); MEMSET(zeros[:], 0.0); MEMSET(zeros_bf[:], 0.0); MEMSET(ones_bf[:], 1.0)
        ASEL(ident[:], ones[:], [[-1, 128]], ALU.is_equal, 0.0, 0, 1)
        ASEL(mU[:], ones[:], [[1, 128]], ALU.is_ge, 0.0, 0, -1)
        ASEL(mUs[:], ones[:], [[1, 128]], ALU.is_gt, 0.0, 0, -1)
        ASEL(mLs[:], ones[:], [[-1, 128]], ALU.is_gt, 0.0, 0, 1)
        ASEL(maskneg[:], zeros[:], [[1, 128]], ALU.is_ge, -30000.0, 0, -1)
        CP(ident_bf[:], ident[:], eng="pool")
        for n in (128, 4):
            MEMSET(lastsel[n][:], 0.0)
            ASEL(lastsel[n][0:n, :], ones[0:n, :], [[0, 128]], ALU.is_equal, 0.0, -(n - 1), 1)
        MEMSET(onespad[:], 0.0)
        MEMSET(onespad[:, 0, 0:64], 1.0); MEMSET(onespad[:, 1, 64:128], 1.0)
        MEMSET(augKpast[:], 1.0)

        mixT = P.sb("mixT", [128, 16, T], BF16)
        AR = [P.sb("ar%d" % i, [128, SLOT]) for i in range(5)]
        RB = [P.sb("rb%d" % i, [128, 512]) for i in range(11)]
        XT = P.sb("xtile", [128, D])
        WB = [P.sb("wb%d" % i, [128, 2048], BF16) for i in range(2)]
        PSR = [P.ps("ps%d" % i) for i in range(4)]
        psO = P.ps("psO"); psD = P.ps("psD"); psS = [P.ps("psS0"), P.ps("psS1")]
        cols = P.sb("cols", [128, 16])
        rr = dict(w=0, ps=0, s=0)

        def PS():
            rr["ps"] += 1
            return PSR[rr["ps"] % 4]

        def PSS():
            rr["s"] += 1
            return psS[rr["s"] % 2]

        class WV:
            def __init__(self, view, keys):
                self.view = view; self.keys = keys

            def __getitem__(self, idx):
                return V(self.view[idx], self.keys)

        def Wload(src, a, b, buf=None):
            if buf is None:
                rr["w"] += 1
                buf = WB[rr["w"] % 2]
            view = buf.t[:, 0:a * b].rearrange("p (a b) -> p a b", b=b)
            P.dma(V(view, buf[:].keys), DV(src), eng="pool")
            return WV(view, buf[:].keys)

        def Win(l, c0, w):
            return Wload(I["win"][l, :, c0:c0 + w].rearrange("(k p) c -> p k c", p=128), 8, w)

        def arbf(i):
            return V(AR[i].t[:].bitcast(BF16), AR[i][:].keys)

        def mkst(name, NT, n, ntile, CW, nseq, L, sample):
            s = St(); s.name = name; s.NT = NT; s.n = n; s.ntile = ntile; s.CW = CW; s.nch = NT // CW
            s.nseq = nseq; s.L = L; s.sample = sample; s.tpc = CW // n
            s.xT = P.sb(name + "xT", [128, 8, NT], BF16)
            s.x_in = I["xp"] if not sample else I["xs"]
            s.y_out = O["yp"] if not sample else O["ys"]
            s.pfx = "p_" if not sample else "s_"
            s.nlev = 6 if n == 128 else 1
            return s

        STS = {}
        if "p" in streams:
            STS["p"] = mkst("p", T, 128, 16, 512, 1, T, False)
        if "s" in streams:
            STS["s"] = mkst("s", 16, 4, 4, 16, 4, 4, True)

        def prep_tile(st, t, xt):
            n = st.n
            for half in range(2):
                ps = PS()
                for kk in range(4):
                    k = half * 4 + kk
                    P.tr(ps[:, kk * n:(kk + 1) * n], sub(xt, xt.ap[0:n, k * 128:(k + 1) * 128]), ident[0:n, 0:n])
                CP(st.xT[:, half * 4:(half + 1) * 4, t * n:(t + 1) * n],
                   sub(ps[:], ps.t[:, 0:4 * n].rearrange("p (a b) -> p a b", b=n)), eng="act")

        def proj_fm(st, wv, c0, M, ch, ps):
            CW = st.CW
            for k in range(8):
                P.mm(ps[0:M, 0:CW], wv[:, k, c0:c0 + M], st.xT[:, k, ch * CW:(ch + 1) * CW], start=(k == 0), stop=(k == 7))

        def proj_tm(st, wv, c0, w, t, psv):
            n = st.n
            for k in range(8):
                P.mm(psv, st.xT[:, k, t * n:(t + 1) * n], wv[:, k, c0:c0 + w], start=(k == 0), stop=(k == 7))

        def silu_from(ps_v, tmp_v, out_v):
            ACTF(tmp_v, ps_v, AF.Exp, scale=-1.0)
            TS(tmp_v, tmp_v, 1.0, ALU.add)
            RECIP(tmp_v, tmp_v)
            TT(out_v, ps_v, tmp_v, ALU.mult)

        def zero_init(bank, ncols):
            P.mm(bank[:, 0:ncols], zeros_bf[:, 0:128], zeros_bf[:, 0:ncols], start=True, stop=False)

        def attn_epilogue(st, l, gate_c0, ch, mix_e, ocol0):
            CW = st.CW
            wv = Win(l, gate_c0, 128)
            pg = PS()
            proj_fm(st, wv, 0, 128, ch, pg)
            sg = RB[0][:, 0:CW]; tmp = RB[1][:, 0:CW]; rd = RB[2][:, 0:CW]
            silu_from(pg[:, 0:CW], tmp, sg)
            RECIP(rd, psD[:, ocol0:ocol0 + CW])
            TT(rd, rd, sg, ALU.mult)
            TT(mixT[:, mix_e, ch * CW:(ch + 1) * CW], psO[:, ocol0:ocol0 + CW], rd, ALU.mult)

        memT = P.sb("memT", [128, 8, 256], BF16)
        KTm = P.sb("KTm", [128, 4, 256], BF16)
        Vm = P.sb("Vm", [128, 2, 512], BF16)

        def mem_ctx_from_tok(ktok, vtok):
            for mt in range(2):
                ps = PS()
                for h in range(4):
                    kv = ktok(mt)
                    P.tr(ps[:, h * 128:(h + 1) * 128], sub(kv, kv.ap[:, h * 128:(h + 1) * 128]), ident[:])
                CP(KTm[:, :, mt * 128:(mt + 1) * 128], sub(ps[:], ps.t[:, :].rearrange("p (h m) -> p h m", m=128)), eng="act")
                CP(Vm[:, mt, :], vtok(mt), eng="pool")

        def group_D(st, l):
            n, CW = st.n, st.CW
            QdT = arbf(0)
            if not st.sample:
                kst = [RB[3], RB[4]]; vst = [RB[5], RB[6]]
                for ech in range(4):
                    wv = Wload(I["wmem"][l, :, ech * 256:(ech + 1) * 256].rearrange("(k p) c -> p k c", p=128), 8, 256)
                    for mt in range(2):
                        ps = PS()
                        for k in range(8):
                            P.mm(ps[:, 0:256], memT[:, k, mt * 128:(mt + 1) * 128], wv[:, k, :], start=(k == 0), stop=(k == 7))
                        dst = (kst if ech < 2 else vst)[mt]
                        CP(dst[:, (ech % 2) * 256:(ech % 2 + 1) * 256], ps[:, 0:256], eng="act")
                for mt in range(2):
                    P.dma(DV(O["p_mk"][l, mt * 128:(mt + 1) * 128, :]), kst[mt][:])
                    P.dma(DV(O["p_mv"][l, mt * 128:(mt + 1) * 128, :]), vst[mt][:])
                mem_ctx_from_tok(lambda mt: kst[mt][:], lambda mt: vst[mt][:])
            for h in range(4):
                wv = Win(l, OFF["d_q"] + h * 128, 128)
                for ch in range(st.nch):
                    ps = PS()
                    proj_fm(st, wv, 0, 128, ch, ps)
                    CP(sub(QdT, QdT.ap[:, ch * CW:(ch + 1) * CW]), ps[:, 0:CW], eng="act")
                if not st.sample:
                    ctxs = [(None, ch, ch * CW, CW) for ch in range(st.nch)]
                else:
                    ctxs = [(s, 0, s * 4, 4) for s in range(4)]
                for (s, ch, q0, qw) in ctxs:
                    if s is not None and h == 0:
                        pass
                    if s is not None:
                        for mt in range(2):
                            P.dma(RB[3 + mt][:], DV(I["mk"][l, s, mt * 128:(mt + 1) * 128, :]))
                            P.dma(RB[5 + mt][:], DV(I["mv"][l, s, mt * 128:(mt + 1) * 128, :]))
                        mem_ctx_from_tok(lambda mt: RB[3 + mt][:], lambda mt: RB[5 + mt][:])
                    oc = 0 if s is None else q0
                    if s is None or s == 0:
                        zero_init(psO, CW); zero_init(psD, CW)
                    for mt in range(2):
                        pS = PSS()
                        P.mm(pS[:, 0:qw], KTm[:, h, mt * 128:(mt + 1) * 128], sub(QdT, QdT.ap[:, q0:q0 + qw]))
                        pt_ = V(RB[7].t[:].bitcast(BF16)[:, 0:qw], RB[7][:].keys)
                        ACTF(pt_, pS[:, 0:qw], AF.Exp, scale=128.0 ** -0.5)
                        P.mm(psO[:, oc:oc + qw], Vm[:, mt, h * 128:(h + 1) * 128], pt_, start=False, stop=False, skip_group_check=True)
                        P.mm(psD[:, oc:oc + qw], ones_bf[:, :], pt_, start=False, stop=False, skip_group_check=True)
                    if s is None:
                        attn_epilogue(st, l, OFF["d_gate"] + h * 128, ch, 12 + h, 0)
                if st.sample:
                    attn_epilogue(st, l, OFF["d_gate"] + h * 128, 0, 12 + h, 0)

        def finish_layer(st, l, last):
            n = st.n
            wo = []
            for g2 in range(2):
                for dch in range(2):
                    i = g2 * 2 + dch
                    view = AR[i].t[:].bitcast(BF16)[:, 0:4096].rearrange("p (a b) -> p a b", b=512)
                    P.dma(V(view, AR[i][:].keys),
                          DV(I["wout"][l, g2 * 1024:(g2 + 1) * 1024, dch * 512:(dch + 1) * 512].rearrange("(e p) d -> p e d", p=128)),
                          eng="pool")
                    wo.append(WV(view, AR[i][:].keys))
            lg = AR[4][:, 0:1024]; lb = AR[4][:, 1024:2048]
            P.dma(lg, DV(I["lng"][l, 0:1, :].broadcast_to([128, D])))
            P.dma(lb, DV(I["lnb"][l, 0:1, :].broadcast_to([128, D])))
            junk = V(RB[0].t[:].bitcast(BF16), RB[0][:].keys)
            for t in range(st.ntile):
                rows = slice(t * n, (t + 1) * n)
                xin = XT[0:n, :]
                src = st.x_in[rows, :] if l == 0 else st.y_out[rows, :]
                P.dma(xin, DV(src, (st.name + "y", t)))
                for dch in range(2):
                    ps = PS()
                    for e in range(16):
                        P.mm(ps[0:n, :], mixT[:, e, rows], wo[(e // 8) * 2 + dch][:, e % 8, :], start=(e == 0), stop=(e == 15))
                    STT(sub(xin, XT.t[0:n, dch * 512:(dch + 1) * 512]), sub(xin, XT.t[0:n, dch * 512:(dch + 1) * 512]), ALPHA, ps[0:n, :], ALU.mult, ALU.add)
                c = lambda i: cols[0:n, i:i + 1]
                RSUM(c(0), xin)
                TS(c(1), c(0), -1.0 / D, ALU.mult)
                TS(xin, xin, c(1), ALU.add)
                ACTF(sub(junk, junk.ap[0:n, 0:D]), xin, AF.Square, accum=c(2))
                ACTF(c(3), c(2), AF.Ln, scale=1.0 / D, bias=EPS)
                ACTF(c(3), c(3), AF.Exp, scale=-0.5)
                STT(xin, xin, c(3), sub(lg, AR[4].t[0:n, 0:1024]), ALU.mult, ALU.mult)
                TT(xin, xin, sub(lb, AR[4].t[0:n, 1024:2048]), ALU.add)
                P.dma(DV(st.y_out[rows, :], (st.name + "y", t)), xin)
                if not last:
                    prep_tile(st, t, xin)

        lf_tok = P.sb("lf_tok", [128, 16, 8]); c_tok = P.sb("c_tok", [128, 16, 8]); negc = P.sb("negc", [128, 16, 8])
        cp32 = P.sb("cp32", [128, 16, 8, 3]); cpb = P.sb("cpb", [128, 16, 8], BF16)
        fbias = P.sb("fbias", [128, 8])
        augq = P.sb("augq", [3, 512], BF16)
        ptbuf = [P.sb("ptb%d" % i, [128, 512], BF16) for i in range(2)]
        idx = P.sb("idx", [128, 16], I32); ptb = P.sb("ptbc", [128, 16], I32); pmod = P.sb("pmod", [128, 1])
        Lg = Buf("rb9", RB[9].t[:].rearrange("p (g x) -> p g x", x=32)); Rt = Buf("rb10", RB[10].t[:].rearrange("p (g x) -> p g x", x=32))
        KTp = [P.sb("KTp%d" % i, [128, 4, 128], BF16) for i in range(2)]
        Vpp = [P.sb("Vpp%d" % i, [128, 4, 2, 128], BF16) for i in range(2)]
        for b in Vpp:
            MEMSET(b[:], 0.0)
        for q in range(4):
            P.op("pool", lambda e, q=q: e.iota(pmod.t[32 * q:32 * (q + 1), :], [[0, 1]], base=0, channel_multiplier=1,
                                              allow_small_or_imprecise_dtypes=True), w=[pmod[:]])
        P_kg = [AR[1], AR[2]]
        P_vg = [AR[3], AR[4]]

        def fox_scalars(st, l):
            n, nt = st.n, st.ntile
            wv = Win(l, OFF["c_f"], 8)
            ps = PS()
            for t in range(nt):
                proj_tm(st, wv, 0, 8, t, ps[0:n, t * 8:(t + 1) * 8])
            P.dma(fbias[:], DV(I["ffb"][l, 0:1, :].broadcast_to([128, 8])))
            x = RB[0][0:n, 0:nt * 8]
            x3 = sub(x, RB[0].t[0:n, 0:nt * 8].rearrange("p (t h) -> p t h", h=8))
            TT(x3, sub(ps[:], ps.t[0:n, 0:nt * 8].rearrange("p (t h) -> p t h", h=8)),
               sub(fbias[:], fbias.t[0:n, :].unsqueeze(1).broadcast_to([n, nt, 8])), ALU.add)
            ACTF(x, x, AF.Exp, scale=-1.0)
            ACTF(x, x, AF.Ln, bias=1.0)
            lf = lf_tok[0:n, 0:nt, :]
            TS(lf, x3, -1.0, ALU.mult)
            P.dma(DV(O[st.pfx + "fl"][l].rearrange("(t p) h -> p t h", p=n)), lf)
            ps2 = PS()
            lf2 = sub(lf, lf_tok.t[0:n, 0:nt, :].rearrange("p t h -> p (t h)"))
            P.mm(ps2[0:n, 0:nt * 8], mU[0:n, 0:n], lf2)
            c2 = sub(c_tok[:], c_tok.t[0:n, 0:nt, :].rearrange("p t h -> p (t h)"))
            if st.sample:
                CP(c2, ps2[0:n, 0:nt * 8])
            else:
                ps3 = PS()
                P.mm(ps3[0:n, 0:nt * 8], ones[0:n, 0:n], lf2)
                tot = RB[1][0:n, 0:nt * 8]
                CP(sub(tot, RB[1].t[0:n, 0:nt * 8].rearrange("p (h t) -> p t h", t=nt)),
                   sub(ps3[:], ps3.t[0:n, 0:nt * 8].rearrange("p (t h) -> p t h", h=8)))
                rm = RB[2][0:n, 0:nt * 8]
                MEMSET(rm, 1.0)
                MEMSET(sub(rm, RB[2].t[0:n, 0:nt * 8].rearrange("p (h t) -> p h t", t=nt)[:, :, 0:1]), 0.0)
                inc = RB[3][0:n, 0:nt * 8]
                P.op("dve", lambda e: e.tensor_tensor_scan(inc.ap, rm.ap, tot.ap, 0.0, ALU.mult, ALU.add), w=[inc], r=[rm, tot])
                TT(inc, inc, tot, ALU.subtract)
                TT(sub(c_tok[:], c_tok.t[0:n, 0:nt, :]), sub(ps2[:], ps2.t[0:n, 0:nt * 8].rearrange("p (t h) -> p t h", h=8)),
                   sub(inc, RB[3].t[0:n, 0:nt * 8].rearrange("p (h t) -> p t h", t=nt)), ALU.add)
            TS(sub(negc[:], negc.t[0:n, 0:nt, :].rearrange("p t h -> p (t h)")), c2, -1.0, ALU.mult)
            cb = sub(cpb[:], cpb.t[0:n, 0:nt, :])
            c3 = sub(c_tok[:], c_tok.t[0:n, 0:nt, :])
            res = sub(RB[4][:], RB[4].t[0:n, 0:nt * 8].rearrange("p (t h) -> p t h", h=8))
            CP(res, c3)
            for pc in range(3):
                CP(cb, res)
                pcv = sub(cp32[:], cp32.t[0:n, 0:nt, :, pc])
                CP(pcv, cb)
                if pc < 2:
                    TT(res, res, pcv, ALU.subtract)

        def build_augq(st, h, q0, qw):
            n = st.n
            ps = PS()
            for i in range(qw // n):
                t = q0 // n + i
                P.tr(ps[0:3, i * n:(i + 1) * n], sub(cp32[:], cp32.t[0:n, t, h, :]), ident[0:n, 0:n])
            CP(augq[0:3, 0:qw], ps[0:3, 0:qw], eng="act")

        def ktile_step(nk, heads, qw, KT_of, aug_k, vpad_of, qT_of, bias_v, diag_j0, ocol_of, q_lo=0, prompt_bias=None):
            pS = PSS()
            hb = len(heads)
            for hi, h in enumerate(heads):
                c0 = hi * qw
                P.mm(pS[0:nk, c0 + q_lo:c0 + qw], KT_of(h), qT_of(h, q_lo), start=True, stop=False, skip_group_check=True)
                if diag_j0 is not None:
                    P.mm(pS[0:nk, c0 + q_lo:c0 + q_lo + nk], ident_bf[0:nk, 0:nk], maskneg[0:nk, 0:nk], start=False, stop=False, skip_group_check=True)
                P.mm(pS[0:nk, c0 + q_lo:c0 + qw], aug_k, sub(augq[:], augq.t[0:3, hi * qw + q_lo:hi * qw + qw]) if prompt_bias is not None else aug_q_all(hi, qw, q_lo),
                     start=False, stop=True, skip_group_check=True)
            rr["pt"] = rr.get("pt", 0) + 1
            ptv = ptbuf[rr["pt"] % 2]
            if prompt_bias is not None:
                pt_ = ptv[0:nk, q_lo:qw]
                ACTF(pt_, pS[0:nk, q_lo:qw], AF.Exp, bias=prompt_bias)
            else:
                tmp = RB[8][0:nk, 0:hb * qw]
                TT(sub(tmp, RB[8].t[0:nk, 0:hb * qw].rearrange("p (h q) -> p h q", q=qw)),
                   sub(pS[:], pS.t[0:nk, 0:hb * qw].rearrange("p (h q) -> p h q", q=qw)), bias_v, ALU.add)
                pt_ = ptv[0:nk, 0:hb * qw]
                ACTF(pt_, tmp, AF.Exp)
            for hi, h in enumerate(heads):
                c0 = hi * qw
                oc = ocol_of(h)
                rhs = sub(ptv[:], ptv.t[0:nk, c0 + q_lo:c0 + qw])
                P.mm(psO[:, oc + q_lo:oc + qw], vpad_of(h), rhs, start=False, stop=False, skip_group_check=True)
                P.mm(psD[:, oc + q_lo:oc + qw], onespad[0:nk, h % 2, :], rhs, start=False, stop=False, skip_group_check=True)

        augq_s = P.sb("augq_s", [3, 32], BF16)

        def aug_q_all(hi, qw, q_lo):
            return sub(augq_s[:], augq_s.t[0:3, hi * qw + q_lo:hi * qw + qw])

        def group_C(st, l):
            n, CW, nt = st.n, st.CW, st.ntile
            fox_scalars(st, l)
            if st.sample:
                Vp = [V(RB[4 + pr].t[:].bitcast(BF16), RB[4 + pr][:].keys) for pr in range(4)]
                for pr in range(4):
                    MEMSET(RB[4 + pr][:], 0.0)
            else:
                Vp = [arbf(1 + pr) for pr in range(4)]
                for pr in range(4):
                    MEMSET(AR[1 + pr][:], 0.0)
            for half in range(2):
                wk = Win(l, OFF["c_k"] + half * 256, 256)
                wvv = Win(l, OFF["c_v"] + half * 256, 256)
                for t in range(nt):
                    rows = slice(t * n, (t + 1) * n)
                    ps = PS(); proj_tm(st, wk, 0, 256, t, ps[0:n, 0:256])
                    CP(RB[9][0:n, 0:256], ps[0:n, 0:256], eng="act")
                    P.dma(DV(O[st.pfx + "fk"][l, rows, half * 256:(half + 1) * 256]), RB[9][0:n, 0:256])
                    ps = PS(); proj_tm(st, wvv, 0, 256, t, ps[0:n, 0:256])
                    CP(RB[10][0:n, 0:256], ps[0:n, 0:256], eng="act")
                    P.dma(DV(O[st.pfx + "fv"][l, rows, half * 256:(half + 1) * 256]), RB[10][0:n, 0:256])
                    for pp in range(2):
                        pr = half * 2 + pp
                        dst = Vp[pr].ap[0:n, t * 256:(t + 1) * 256].rearrange("p (b d) -> p b d", d=64)[:, 0::3, :]
                        srcv = RB[10].t[0:n, pp * 128:(pp + 1) * 128].rearrange("p (b d) -> p b d", d=64)
                        CP(V(dst, Vp[pr].keys), V(srcv, RB[10][:].keys), eng="pool")
            QT = arbf(0)
            if st.sample:
                for pr in range(4):
                    for which, off in (("c_q", 0), ("c_k", 64)):
                        wv = Win(l, OFF[which] + pr * 128, 128)
                        ps = PS(); proj_fm(st, wv, 0, 128, 0, ps)
                        if which == "c_q":
                            ACTF(sub(QT, QT.ap[:, off + pr * 16: off + pr * 16 + 16]), ps[:, 0:16], AF.Copy, scale=0.125)
                        else:
                            CP(sub(QT, QT.ap[:, off + pr * 16: off + pr * 16 + 16]), ps[:, 0:16], eng="act")
                zero_init(psO, 64); zero_init(psD, 64)
                for s in range(4):
                    fox_sample_seq(st, l, s, QT, Vp)
                for pr in range(4):
                    attn_epilogue(st, l, OFF["c_gate"] + pr * 128, 0, 8 + pr, pr * 16)
                return
            for pr in range(4):
                for which, off in (("c_q", 0), ("c_k", T)):
                    wv = Win(l, OFF[which] + pr * 128, 128)
                    for ch in range(st.nch):
                        ps = PS(); proj_fm(st, wv, 0, 128, ch, ps)
                        dst = sub(QT, QT.ap[:, off + ch * CW: off + (ch + 1) * CW])
                        if which == "c_q":
                            ACTF(dst, ps[:, 0:CW], AF.Copy, scale=0.125)
                        else:
                            CP(dst, ps[:, 0:CW], eng="act")
                for ch in range(st.nch):
                    zero_init(psO, CW); zero_init(psD, CW)
                    for hp in range(2):
                        h = pr * 2 + hp
                        build_augq(st, h, ch * CW, CW)
                        rws = slice(hp * 64, (hp + 1) * 64)
                        for kt in range(ch * 4 + 4):
                            j = kt - ch * 4
                            q_lo = max(j, 0) * 128
                            ktile_step(
                                128, [h], CW,
                                KT_of=lambda hh: sub(QT, QT.ap[rws, T + kt * 128: T + (kt + 1) * 128]),
                                aug_k=ones_bf[0:3, 0:128],
                                vpad_of=lambda hh: sub(Vp[pr], Vp[pr].ap[:, kt * 256 + hp * 128: kt * 256 + (hp + 1) * 128]),
                                qT_of=lambda hh, ql: sub(QT, QT.ap[rws, ch * CW + ql:(ch + 1) * CW]),
                                bias_v=None, diag_j0=(j if j >= 0 else None), ocol_of=lambda hh: 0, q_lo=q_lo,
                                prompt_bias=negc[:, kt, h:h + 1])
                    attn_epilogue(st, l, OFF["c_gate"] + pr * 128, ch, 8 + pr, 0)

        def fox_sample_seq(st, l, s, QT, Vp):
            for q in range(4):
                P.dma(ptb[32 * q:32 * (q + 1), :], DV(I["pt"][s:s + 1, q::4].broadcast_to([32, 16])), allow_slow_non_contiguous=True)
            TS(idx[:], ptb[:], 32.0, ALU.mult, pmod[:, 0:1], ALU.add)
            for g in range(16):
                P.op("pool", lambda e, g=g: e.indirect_dma_start(
                    out=Lg.t[:, g, :], out_offset=None, in_=I["fl"],
                    in_offset=bass.IndirectOffsetOnAxis(ap=idx.t[:, g:g + 1], axis=0), element_offset=l * 81920 * 32),
                    w=[Lg[:]], r=[idx[:]], kind="d")
            L4 = sub(Lg[:], Lg.t[:].rearrange("p g (r h) -> p g r h", h=8))
            rs = RB[0][:, 0:128]
            P.op("dve", lambda e: e.tensor_reduce(RB[0].t[:, 0:128].rearrange("p (h g) -> p g h", g=16),
                                                  Lg.t[:].rearrange("p g (r h) -> p g h r", h=8), AX.X, ALU.add),
                 w=[rs], r=[Lg[:]])
            rm = RB[1][:, 0:128]
            MEMSET(rm, 1.0)
            MEMSET(sub(rm, RB[1].t[:, 0:128].rearrange("p (h g) -> p h g", g=16)[:, :, 0:1]), 0.0)
            pre = RB[2][:, 0:128]
            P.op("dve", lambda e: e.tensor_tensor_scan(pre.ap, rm.ap, rs.ap, 0.0, ALU.mult, ALU.add), w=[pre], r=[rm, rs])
            rs2 = RB[3][:, 0:128]
            TT(sub(rs2, RB[3].t[:, 0:128].rearrange("p (h g) -> p h g", g=16)),
               sub(pre, RB[2].t[:, 0:128].rearrange("p (h g) -> p h g", g=16)[:, :, 15:16].broadcast_to([128, 8, 16])),
               sub(pre, RB[2].t[:, 0:128].rearrange("p (h g) -> p h g", g=16)), ALU.subtract)
            ps = PS()
            P.mm(ps[:, 0:128], mLs[:, :], rs, start=True, stop=False)
            P.mm(ps[:, 0:128], ones[:, :], rs2, start=False, stop=True)
            R4 = sub(Rt[:], Rt.t[:].rearrange("p g (r h) -> p g r h", h=8))
            MEMSET(Rt[:], 0.0)
            for r in (2, 1, 0):
                TT(sub(Rt[:], R4.ap[:, :, r, :]), sub(Rt[:], R4.ap[:, :, r + 1, :]), sub(Lg[:], L4.ap[:, :, r + 1, :]), ALU.add)
            TT(R4, R4, sub(ps[:], ps.t[:, 0:128].rearrange("p (h g) -> p g h", g=16).unsqueeze(2).broadcast_to([128, 16, 4, 8])), ALU.add)
            psq = PS()
            for h in range(8):
                P.tr(psq[0:3, h * 4:(h + 1) * 4], sub(cp32[:], cp32.t[0:4, s, h, :]), ident[0:4, 0:4])
            CP(augq_s[:], psq[0:3, 0:32], eng="act")
            heads = list(range(8))
            qT_of = lambda h, ql: sub(QT, QT.ap[(h % 2) * 64:(h % 2) * 64 + 64, (h // 2) * 16 + s * 4:(h // 2) * 16 + s * 4 + 4])
            ocol_of = lambda h: (h // 2) * 16 + s * 4
            for g in range(16):
                kg = P_kg[g % 2]; vg = P_vg[g % 2]
                P.op("pool", lambda e, kg=kg, g=g: e.indirect_dma_start(
                    out=kg.t[:, 0:2048], out_offset=None, in_=I["fk"],
                    in_offset=bass.IndirectOffsetOnAxis(ap=idx.t[:, g:g + 1], axis=0), element_offset=l * 81920 * 2048), w=[kg[:]], r=[idx[:]], kind="d")
                P.op("pool", lambda e, vg=vg, g=g: e.indirect_dma_start(
                    out=vg.t[:, 0:2048], out_offset=None, in_=I["fv"],
                    in_offset=bass.IndirectOffsetOnAxis(ap=idx.t[:, g:g + 1], axis=0), element_offset=l * 81920 * 2048), w=[vg[:]], r=[idx[:]], kind="d")
                for r in range(4):
                    kt = g * 4 + r
                    ktp = KTp[kt % 2]; vpp = Vpp[kt % 2]
                    pst = PS()
                    for pr in range(4):
                        P.tr(pst[:, pr * 128:(pr + 1) * 128], sub(kg[:], kg.t[:, r * 512 + pr * 128: r * 512 + (pr + 1) * 128]), ident[:])
                    CP(ktp[:], sub(pst[:], pst.t[:, :].rearrange("p (a b) -> p a b", b=128)), eng="act")
                    CP(V(vpp.t[:].rearrange("p a b (x d) -> p a (b x) d", d=64)[:, :, 0::3, :], vpp[:].keys),
                       sub(vg[:], vg.t[:, r * 512:(r + 1) * 512].rearrange("p (a b d) -> p a b d", b=2, d=64)), eng="pool")
                    ktile_step(
                        128, heads, 4,
                        KT_of=lambda h: sub(ktp[:], ktp.t[(h % 2) * 64:(h % 2) * 64 + 64, h // 2, :]),
                        aug_k=augKpast[0:3, :],
                        vpad_of=lambda h: sub(vpp[:], vpp.t[:, h // 2, h % 2, :]),
                        qT_of=qT_of,
                        bias_v=sub(Rt[:], R4.ap[:, g, r, :].unsqueeze(2).broadcast_to([128, 8, 4])),
                        diag_j0=None, ocol_of=ocol_of)
            ktile_step(
                4, heads, 4,
                KT_of=lambda h: sub(QT, QT.ap[(h % 2) * 64:(h % 2) * 64 + 64, 64 + (h // 2) * 16 + s * 4: 64 + (h // 2) * 16 + s * 4 + 4]),
                aug_k=ones_bf[0:3, 0:4],
                vpad_of=lambda h: sub(Vp[h // 2], Vp[h // 2].ap[0:4, s * 256 + (h % 2) * 128: s * 256 + (h % 2 + 1) * 128]),
                qT_of=qT_of,
                bias_v=sub(negc[:], negc.t[0:4, s, :].unsqueeze(2).broadcast_to([4, 8, 4])),
                diag_j0=0, ocol_of=ocol_of)

        fld = P.sb("fld", [128, 8, 128])
        gl = P.sb("gl", [128, 2, 128])
        hb_ = P.sb("hbc", [128, 4, 8])
        cw = P.sb("cw", [128, 8])
        Sst = P.sb("Sst", [128, 128])
        nwc = P.sb("nwc", [128, 4])
        F_BETA, F_G, F_GAM, F_BE, F_ET, F_NB, F_DT, F_SS = range(8)

        def fv(f, n, nt, H):
            return sub(fld[:], fld.t[0:n, f, 0:nt * H])

        def fv3(f, n, nt, H):
            return sub(fld[:], fld.t[0:n, f, 0:nt * H].rearrange("p (t h) -> p t h", h=H))

        def fcol(f, n, t0, h, H, shape):
            return sub(fld[:], fld.t[0:n, f, :].rearrange("p (t h) -> p t h", h=H)[:, t0:t0 + 4, h:h + 1].broadcast_to(shape))

        def conv_fm(st, l, c0, dst, wname, bname, sname, oname, cch):
            n, CW, L, ns = st.n, st.CW, st.L, st.nseq
            U = sub(AR[0][:], AR[0].t[:, 0:ns * (L + 3)].rearrange("p (s x) -> p s x", x=L + 3))
            if st.sample:
                for sq in range(ns):
                    P.dma(sub(U, U.ap[:, sq, 0:3]), DV(I[sname][l, sq, :, cch:cch + 128].rearrange("r c -> c r")), allow_slow_non_contiguous=True)
            else:
                MEMSET(sub(U, U.ap[:, :, 0:3]), 0.0)
            P.dma(cw[:, 0:4], DV(I[wname][l, :, cch:cch + 128].rearrange("i c -> c i")), allow_slow_non_contiguous=True)
            P.dma(cw[:, 4:5], DV(I[bname][l, 0:1, cch:cch + 128].rearrange("o c -> c o")), allow_slow_non_contiguous=True)
            wv = Win(l, c0, 128)
            for ch in range(st.nch):
                ps = PS(); proj_fm(st, wv, 0, 128, ch, ps)
                if st.sample:
                    CP(sub(U, U.ap[:, :, 3:3 + L]), sub(ps[:], ps.t[:, 0:16].rearrange("p (s x) -> p s x", x=L)), eng="act")
                else:
                    CP(sub(U, U.ap[:, 0, 3 + ch * CW:3 + (ch + 1) * CW]), ps[:, 0:CW], eng="act")
            od = O[st.pfx + oname]
            if st.sample:
                for sq in range(ns):
                    P.dma(DV(od[l, sq, :, cch:cch + 128].rearrange("r c -> c r")), sub(U, U.ap[:, sq, L:L + 3]), allow_slow_non_contiguous=True)
            else:
                P.dma(DV(od[l, :, cch:cch + 128].rearrange("r c -> c r")), sub(U, U.ap[:, 0, L:L + 3]), allow_slow_non_contiguous=True)
            Y = sub(dst, dst.ap[:, 0:ns * L].rearrange("p (s x) -> p s x", x=L))
            TS(Y, sub(U, U.ap[:, :, 3:3 + L]), cw[:, 3:4], ALU.mult, cw[:, 4:5], ALU.add)
            for i in range(3):
                STT(Y, sub(U, U.ap[:, :, i:i + L]), cw[:, i:i + 1], Y, ALU.mult, ALU.add)
            tmp = sub(AR[4][:], AR[4].t[:, 0:ns * L].rearrange("p (s x) -> p s x", x=L))
            silu_from(Y, tmp, Y)

        def tok_scalars(st, l, c0, ncols, H, is_gdn):
            n, nt = st.n, st.ntile
            wv = Win(l, c0, ncols)
            ps = PS()
            for t in range(nt):
                proj_tm(st, wv, 0, ncols, t, ps[0:n, t * ncols:(t + 1) * ncols])
            raw = sub(ps[:], ps.t[0:n, 0:nt * ncols].rearrange("p (t c) -> p t c", c=ncols))
            bc = lambda k: sub(hb_[:], hb_.t[0:n, k, 0:H].unsqueeze(1).broadcast_to([n, nt, H]))
            x = sub(RB[0][:], RB[0].t[0:n, 0:nt * H].rearrange("p (t h) -> p t h", h=H))
            x2 = RB[0][0:n, 0:nt * H]
            if is_gdn:
                P.dma(hb_[:, 0, 0:4], DV(I["gdb"][l, 0:1, :].broadcast_to([128, 4])))
                P.dma(hb_[:, 1, 0:4], DV(I["gal"][l, 0:1, :].broadcast_to([128, 4])))
                ACTF(x, sub(raw, raw.ap[:, :, 0:4]), AF.Exp, scale=-1.0)
                TS(x2, x2, 1.0, ALU.add)
                RECIP(fv(F_BETA, n, nt, H), x2)
                TS(fv(F_NB, n, nt, H), fv(F_BETA, n, nt, H), -1.0, ALU.mult)
                TT(x, sub(raw, raw.ap[:, :, 4:8]), bc(0), ALU.add)
            else:
                P.dma(hb_[:, 0, 0:8], DV(I["sdb"][l, 0:1, :].broadcast_to([128, 8])))
                P.dma(hb_[:, 1, 0:8], DV(I["sal"][l, 0:1, :].broadcast_to([128, 8])))
                P.dma(hb_[:, 2, 0:8], DV(I["sd"][l, 0:1, :].broadcast_to([128, 8])))
                TT(x, raw, bc(0), ALU.add)
            ACTF(x2, x2, AF.Exp)
            ACTF(x2, x2, AF.Ln, bias=1.0)
            ACTF(hb_[:, 1, 0:H], hb_[:, 1, 0:H], AF.Exp)
            TS(hb_[:, 1, 0:H], hb_[:, 1, 0:H], -1.0, ALU.mult)
            if not is_gdn:
                CP(fv(F_DT, n, nt, H), x2)
            TT(fv3(F_G, n, nt, H), x, bc(1), ALU.mult)
            ps2 = PS()
            P.mm(ps2[0:n, 0:nt * H], mU[0:n, 0:n], fv(F_G, n, nt, H))
            CP(fv(F_GAM, n, nt, H), ps2[0:n, 0:nt * H])
            ps3 = PS()
            P.mm(ps3[:, 0:nt * H], lastsel[n][0:n, :], fv(F_GAM, n, nt, H))
            CP(gl[:, 0, 0:nt * H], ps3[:, 0:nt * H])
            ACTF(gl[:, 1, 0:nt * H], ps3[:, 0:nt * H], AF.Exp)
            TT(fv(F_ET, n, nt, H), sub(gl[:], gl.t[0:n, 0, 0:nt * H]), fv(F_GAM, n, nt, H), ALU.subtract)
            ACTF(fv(F_ET, n, nt, H), fv(F_ET, n, nt, H), AF.Exp)
            if is_gdn:
                ACTF(x2, fv(F_GAM, n, nt, H), AF.Exp)
                TT(fv(F_BE, n, nt, H), x2, fv(F_BETA, n, nt, H), ALU.mult)

        def recur_head(st, l, h, H, dk, dv, KT_of, QT_of, VT_of, kbase, vbase, is_gdn, after_o, s_in, s_out):
            n, nt = st.n, st.ntile
            q4 = 4 * n
            kr = slice(kbase, kbase + dk)
            identb = sub(ident[:], ident.t[0:n, 0:n].unsqueeze(1).broadcast_to([n, 4, n]))
            mUb = sub(mU[:], mU.t[0:n, 0:n].unsqueeze(1).broadcast_to([n, 4, n]))
            mLb = sub(mLs[:], mLs.t[0:n, 0:n].unsqueeze(1).broadcast_to([n, 4, n]))
            v3 = lambda rb, w=n: sub(rb[:], rb.t[0:n, 0:4 * w].rearrange("p (q x) -> p q x", x=w))
            v2 = lambda rb, w=n: rb[0:n, 0:4 * w]
            for qd in range(nt // 4):
                t0 = qd * 4
                c0, c1 = t0 * n, (t0 + 4) * n
                TT(v3(RB[0]), identb, fcol(F_GAM, n, t0, h, H, [n, 4, n]), ALU.mult)
                Gps = PS()
                P.mm(Gps[:, 0:q4], ones[0:n, :], v2(RB[0]))
                ACTF(RB[1][:, 0:q4], Gps[:, 0:q4], AF.Exp)
                TT(v3(RB[2]), sub(Gps[:], Gps.t[0:n, 0:q4].rearrange("p (q x) -> p q x", x=n)), fcol(F_GAM, n, t0, h, H, [n, 4, n]), ALU.subtract)
                TS(v2(RB[3]), v2(RB[2]), 0.0, ALU.min)
                ACTF(v2(RB[3]), v2(RB[3]), AF.Exp)
                TT(v3(RB[3]), v3(RB[3]), mUb, ALU.mult)
                psQ = PS()
                for q in range(4):
                    a, b = c0 + q * n, c0 + (q + 1) * n
                    P.mm(psQ[0:n, q * n:(q + 1) * n], KT_of(a, b), QT_of(a, b))
                TT(v2(RB[6]), psQ[0:n, 0:q4], v2(RB[3]), ALU.mult)
                if is_gdn:
                    TS(v2(RB[4]), v2(RB[2]), 0.0, ALU.max)
                    ACTF(v2(RB[4]), v2(RB[4]), AF.Exp, scale=-1.0)
                    TT(v3(RB[4]), v3(RB[4]), mLb, ALU.mult)
                    TT(v3(RB[4]), v3(RB[4]), fcol(F_NB, n, t0, h, H, [n, 4, n]), ALU.mult)
                    psK = PS()
                    for q in range(4):
                        a, b = c0 + q * n, c0 + (q + 1) * n
                        P.mm(psK[0:n, q * n:(q + 1) * n], KT_of(a, b), KT_of(a, b))
                    TT(v2(RB[5]), psK[0:n, 0:q4], v2(RB[4]), ALU.mult)
                    psT = PS()
                    for q in range(4):
                        P.tr(psT[0:n, q * n:(q + 1) * n], RB[5][0:n, q * n:(q + 1) * n], ident[0:n, 0:n])
                    CP(v2(RB[7]), psT[0:n, 0:q4])
                    TT(v3(RB[8]), v3(RB[7]), identb, ALU.add)
                    X, XTb, Xn, XTn = RB[7], RB[5], RB[10], RB[9]
                    for lev in range(1, st.nlev + 1):
                        lastlev = (lev == st.nlev)
                        if not lastlev:
                            psA = PS()
                            for q in range(4):
                                sl = slice(q * n, (q + 1) * n)
                                P.mm(psA[0:n, sl], XTb[0:n, sl], X[0:n, sl])
                        psB = PS()
                        for q in range(4):
                            sl = slice(q * n, (q + 1) * n)
                            P.mm(psB[0:n, sl], X[0:n, sl], XTb[0:n, sl])
                        CP(v2(XTn), psB[0:n, 0:q4], eng="act")
                        if not lastlev:
                            CP(v2(Xn), psA[0:n, 0:q4])
                        psC = PS()
                        for q in range(4):
                            sl = slice(q * n, (q + 1) * n)
                            P.mm(psC[0:n, sl], XTn[0:n, sl], RB[8][0:n, sl])
                        TT(v2(RB[8]), v2(RB[8]), psC[0:n, 0:q4], ALU.add)
                        X, XTb, Xn, XTn = Xn, XTn, X, XTb
                if rh < 1:
                    continue
                psKt = PS()
                for q in range(4):
                    a, b = c0 + q * n, c0 + (q + 1) * n
                    P.tr(psKt[0:n, q * dk:(q + 1) * dk], KT_of(a, b), ident[kr, kr])
                psVt = PS()
                vr = slice(vbase, vbase + dv)
                for q in range(4):
                    a, b = c0 + q * n, c0 + (q + 1) * n
                    P.tr(psVt[0:n, q * dv:(q + 1) * dv], VT_of(a, b), ident[vr, vr])
                Kt3 = sub(psKt[:], psKt.t[0:n, 0:4 * dk].rearrange("p (q x) -> p q x", x=dk))
                Vt3 = sub(psVt[:], psVt.t[0:n, 0:4 * dv].rearrange("p (q x) -> p q x", x=dv))
                ktl = sub(RB[4][:], RB[4].t[0:n, 0:512].rearrange("p (q x) -> p q x", x=128))
                if dk < 128:
                    MEMSET(RB[4][:], 0.0)
                TT(sub(ktl, ktl.ap[:, :, kbase:kbase + dk]), Kt3, fcol(F_ET, n, t0, h, H, [n, 4, dk]), ALU.mult)
                if is_gdn:
                    TT(v3(RB[0], 128), Kt3, fcol(F_BE, n, t0, h, H, [n, 4, dk]), ALU.mult)
                    TT(v3(RB[2], 128), Vt3, fcol(F_BETA, n, t0, h, H, [n, 4, dv]), ALU.mult)
                    psW = PS()
                    for q in range(4):
                        P.mm(psW[:, q * n:(q + 1) * n], RB[0][0:n, q * 128:(q + 1) * 128], RB[8][0:n, q * n:(q + 1) * n])
                    CP(RB[5][:, 0:q4], psW[:, 0:q4], eng="act")
                    psV2 = PS()
                    for q in range(4):
                        P.mm(psV2[0:n, q * 128:(q + 1) * 128], RB[8][0:n, q * n:(q + 1) * n], RB[2][0:n, q * 128:(q + 1) * 128])
                    CP(RB[7][0:n, :], psV2[0:n, :])
                else:
                    CP(v3(RB[0], dv), Vt3, eng="act")
                    TT(v3(RB[2], dv), Vt3, fcol(F_DT, n, t0, h, H, [n, 4, dv]), ALU.mult)
                if rh < 2:
                    continue
                qd_ = sub(RB[9][:], RB[9].t[kr, 0:q4])
                TT(qd_, QT_of(c0, c1), sub(RB[1][:], RB[1].t[kr, 0:q4]), ALU.mult)
                if rh < 3:
                    continue
                for q in range(4):
                    t = t0 + q
                    sl = slice(q * n, (q + 1) * n)
                    S = Sst[kr, 0:dv]
                    if (st.sample or t == 0) and rh >= 4:
                        s_in(S, t)
                    if is_gdn:
                        psu = PS()
                        P.mm(psu[0:n, 0:dv], RB[5][:, sl], S)
                        u = RB[1][0:n, 0:dv]
                        TT(u, RB[7][0:n, q * 128:(q + 1) * 128], psu[0:n, 0:dv], ALU.subtract)
                    else:
                        u = RB[2][0:n, q * dv:(q + 1) * dv]
                    pso = PS()
                    P.mm(pso[0:n, 0:dv], sub(RB[9][:], RB[9].t[kr, sl]), S, start=True, stop=False)
                    P.mm(pso[0:n, 0:dv], RB[6][0:n, sl], u, start=False, stop=True)
                    pss = PS()
                    P.mm(pss[:, 0:dv], RB[4][0:n, q * 128:(q + 1) * 128], u)
                    STT(S, S, gl[kr, 1, t * H + h:t * H + h + 1], pss[kr, 0:dv], ALU.mult, ALU.add)
                    after_o(pso, t, q)
                    if (st.sample or t == nt - 1) and rh >= 5:
                        s_out(S, t)

        def group_A(st, l):
            n, nt, CW = st.n, st.ntile, st.CW
            tok_scalars(st, l, OFF["a_beta"], 8, 4, True)
            P.dma(nwc[:, 0:1], DV(I["gnw"][l, :, :]))
            for h in range(4):
                conv_fm(st, l, OFF["a_q"] + h * 128, AR[1][:], "gcw", "gcb", "sgc", "gc", h * 128)
                conv_fm(st, l, OFF["a_k"] + h * 128, AR[2][:], "gcw", "gcb", "sgc", "gc", 512 + h * 128)
                conv_fm(st, l, OFF["a_v"] + h * 128, AR[3][:], "gcw", "gcb", "sgc", "gc", 1024 + h * 128)
                for X, lbias in ((AR[1], -0.5 * float(np.log(128.0))), (AR[2], 0.0)):
                    ACTF(AR[4][:, 0:st.NT], X[:, 0:st.NT], AF.Square)
                    for ch in range(st.nch):
                        cs = slice(ch * CW, (ch + 1) * CW)
                        ps = PS()
                        P.mm(ps[:, 0:CW], ones[:, :], AR[4][:, cs])
                        ACTF(RB[0][:, 0:CW], ps[:, 0:CW], AF.Ln, bias=EPS)
                        ACTF(RB[0][:, 0:CW], RB[0][:, 0:CW], AF.Exp, scale=-0.5, bias=lbias)
                        TT(X[:, cs], X[:, cs], RB[0][:, 0:CW], ALU.mult)

                def s_in(S, t, h=h):
                    if st.sample:
                        P.dma(S, DV(I["sg"][l, t, h]))
                    else:
                        MEMSET(S, 0.0)

                def s_out(S, t, h=h):
                    if st.sample:
                        P.dma(DV(O["s_gs"][l, t, h]), S)
                    else:
                        P.dma(DV(O["p_gs"][l, h]), S)

                def after_o(pso, t, q, h=h):
                    c = lambda i: cols[0:n, i:i + 1]
                    ACTF(RB[3][0:n, 0:128], pso[0:n, 0:128], AF.Square, accum=c(4))
                    ACTF(c(5), c(4), AF.Ln, scale=1.0 / 128, bias=EPS)
                    ACTF(c(5), c(5), AF.Exp, scale=-0.5)
                    TS(RB[3][0:n, 128:256], pso[0:n, 0:128], c(5), ALU.mult)
                    pT = PS()
                    P.tr(pT[:, 0:n], RB[3][0:n, 128:256], ident[0:n, 0:n])
                    ACTF(mixT[:, h, t * n:(t + 1) * n], pT[:, 0:n], AF.Copy, scale=nwc[:, 0:1])

                recur_head(st, l, h, 4, 128, 128,
                           lambda a, b: AR[2][:, a:b], lambda a, b: AR[1][:, a:b], lambda a, b: AR[3][:, a:b],
                           0, 0, True, after_o, s_in, s_out)
                wv = Win(l, OFF["a_gate"] + h * 128, 128)
                for ch in range(st.nch):
                    cs = slice(ch * CW, (ch + 1) * CW)
                    pg = PS(); proj_fm(st, wv, 0, 128, ch, pg)
                    silu_from(pg[:, 0:CW], RB[1][:, 0:CW], RB[0][:, 0:CW])
                    TT(mixT[:, h, cs], mixT[:, h, cs], RB[0][:, 0:CW], ALU.mult)

        def group_B(st, l):
            n, nt, CW = st.n, st.ntile, st.CW
            tok_scalars(st, l, OFF["b_dt"], 8, 8, False)
            MEMSET(fv(F_SS, n, nt, 8), 0.0)
            conv_fm(st, l, OFF["b_B"], AR[1][:], "scw", "scb", "ssc", "sc", 512)
            conv_fm(st, l, OFF["b_C"], AR[2][:], "scw", "scb", "ssc", "sc", 640)
            for pr in range(4):
                conv_fm(st, l, OFF["b_x"] + pr * 128, AR[3][:], "scw", "scb", "ssc", "sc", pr * 128)
                for hp in range(2):
                    h = pr * 2 + hp
                    g = h // 4
                    gr = slice(g * 64, (g + 1) * 64)
                    xr = slice(hp * 64, (hp + 1) * 64)
                    wg = Win(l, OFF["b_gate"] + h * 64, 64)

                    def s_in(S, t, h=h, gr=gr):
                        if st.sample:
                            P.dma(RB[10][0:64, 0:64], DV(I["ss"][l, t, h]))
                            pT = PS()
                            P.tr(pT[0:64, 0:64], RB[10][0:64, 0:64], ident[0:64, 0:64])
                            CP(S, pT[0:64, 0:64], eng="act")
                        else:
                            MEMSET(S, 0.0)

                    def s_out(S, t, h=h, gr=gr):
                        dst = O["s_ss"][l, t, h] if st.sample else O["p_ss"][l, h]
                        pT = PS()
                        P.tr(pT[0:64, 0:64], S, ident[gr, gr])
                        CP(RB[10][0:64, 64:128], pT[0:64, 0:64], eng="act")
                        P.dma(DV(dst), RB[10][0:64, 64:128])

                    def after_o(pso, t, q, h=h, hp=hp, pr=pr, wg=wg):
                        y = RB[3][0:n, 0:64]
                        STT(y, RB[0][0:n, q * 64:(q + 1) * 64], hb_[0:n, 2, h:h + 1], pso[0:n, 0:64], ALU.mult, ALU.add)
                        psg = PS()
                        proj_tm(st, wg, 0, 64, t, psg[0:n, 0:64])
                        silu_from(psg[0:n, 0:64], RB[3][0:n, 64:128], RB[3][0:n, 128:192])
                        z = RB[3][0:n, 192:256]
                        TT(z, y, RB[3][0:n, 128:192], ALU.mult)
                        ACTF(RB[3][0:n, 256:320], z, AF.Square, accum=sub(fld[:], fld.t[0:n, F_SS, t * 8 + h:t * 8 + h + 1]))
                        pT = PS()
                        P.tr(pT[0:64, 0:n], z, ident[0:n, 0:n])
                        CP(mixT[hp * 64:(hp + 1) * 64, 4 + pr, t * n:(t + 1) * n], pT[0:64, 0:n], eng="act")

                    if dbg >= 2:
                        recur_head(st, l, h, 8, 64, 64,
                                   lambda a, b, gr=gr: AR[1][gr, a:b], lambda a, b, gr=gr: AR[2][gr, a:b],
                                   lambda a, b, xr=xr: AR[3][xr, a:b],
                                   g * 64, hp * 64, False, after_o if dbg >= 3 else (lambda *a: None), s_in, s_out)
            if dbg < 4:
                return
            ssv = sub(fld[:], fld.t[0:n, F_SS, 0:nt * 8].rearrange("p (t h) -> p t h", h=8))
            RSUM(cols[0:n, 0:nt], ssv)
            ACTF(cols[0:n, 0:nt], cols[0:n, 0:nt], AF.Ln, scale=1.0 / 512, bias=EPS)
            ACTF(cols[0:n, 0:nt], cols[0:n, 0:nt], AF.Exp, scale=-0.5)
            for e in range(4):
                P.dma(nwc[:, e:e + 1], DV(I["snw"][l, e * 128:(e + 1) * 128, :]))
            for ch in range(st.nch):
                t0 = ch * 4
                d0 = sub(RB[0][:], RB[0].t[0:n, 0:4 * n].rearrange("p (q x) -> p q x", x=n))
                TT(d0, sub(ident[:], ident.t[0:n, 0:n].unsqueeze(1).broadcast_to([n, 4, n])),
                   sub(cols[:], cols.t[0:n, t0:t0 + 4].unsqueeze(2).broadcast_to([n, 4, n])), ALU.mult)
                Rps = PS()
                P.mm(Rps[:, 0:4 * n], ones[0:n, :], RB[0][0:n, 0:4 * n])
                cs = slice(ch * CW, (ch + 1) * CW)
                for e in range(4):
                    STT(mixT[:, 4 + e, cs], mixT[:, 4 + e, cs], nwc[:, e:e + 1], Rps[:, 0:4 * n], ALU.mult, ALU.mult)

        for st in STS.values():
            for t in range(st.ntile):
                xin = XT[0:st.n, :]
                P.dma(xin, DV(st.x_in[t * st.n:(t + 1) * st.n, :]))
                prep_tile(st, t, xin)
        if "p" in STS:
            for mt in range(2):
                P.dma(XT[:, :], DV(I["memp"][mt * 128:(mt + 1) * 128, :]))
                for half in range(2):
                    ps = PS()
                    for kk in range(4):
                        k = half * 4 + kk
                        P.tr(ps[:, kk * 128:(kk + 1) * 128], XT[:, k * 128:(k + 1) * 128], ident[:])
                    CP(memT[:, half * 4:(half + 1) * 4, mt * 128:(mt + 1) * 128],
                       sub(ps[:], ps.t[:, :].rearrange("p (a b) -> p a b", b=128)), eng="act")
        for l in range(nlayers):
            for st in STS.values():
                MEMSET(mixT[:, :, 0:st.NT], 0.0) if l == 0 else None
                if "D" in parts:
                    group_D(st, l)
                if "C" in parts:
                    group_C(st, l)
                if "A" in parts:
                    group_A(st, l)
                if "B" in parts:
                    group_B(st, l)
                finish_layer(st, l, last=(l == nlayers - 1))
        P.emit()
    nc._small_cache = small
    return nc


_NC = {}


def kernel(**inputs):
    key = "full"
    if key not in _NC:
        _NC[key] = build()
    nc = _NC[key]
    f = lambda a: np.ascontiguousarray(a)
    g = inputs
    fk = g["cache_fox_k"].reshape(163840, 2048)
    fv = g["cache_fox_v"].reshape(163840, 2048)
    fl = g["cache_fox_logf"].reshape(163840, 32)
    if getattr(nc, '_small_cache', False):
        fk, fv, fl = fk[:128], fv[:128], fl[:128]
    in_maps = []
    for c in range(8):
        sl = slice(4 * c, 4 * c + 4)
        m = dict(
            xp=f(g["x_prompt"][c]), xs=f(g["x_sample"][sl].reshape(16, D)),
            sgc=f(g["state_gdn_conv"][:, sl]), sg=f(g["state_gdn"][:, sl]),
            ssc=f(g["state_ssd_conv"][:, sl]), ss=f(g["state_ssd"][:, sl]),
            fk=fk, fv=fv, fl=fl,
            mk=f(g["cache_mem_k"][:, sl].reshape(2, 4, 256, 512)), mv=f(g["cache_mem_v"][:, sl].reshape(2, 4, 256, 512)),
            pt=f(g["page_table"][sl]), memp=f(g["mem_prompt"][c]), win=g["w_in"],
            gcw=g["gdn_conv_w"], gcb=g["gdn_conv_b"].reshape(2, 1, 1536), gal=g["gdn_a_log"].reshape(2, 1, 4),
            gdb=g["gdn_dt_bias"].reshape(2, 1, 4), gnw=g["gdn_norm_w"].reshape(2, 128, 1),
            scw=g["ssd_conv_w"], scb=g["ssd_conv_b"].reshape(2, 1, 768), sal=g["ssd_a_log"].reshape(2, 1, 8),
            sdb=g["ssd_dt_bias"].reshape(2, 1, 8), sd=g["ssd_d"].reshape(2, 1, 8), snw=g["ssd_norm_w"].reshape(2, 512, 1),
            ffb=g["fox_f_bias"].reshape(2, 1, 8), wmem=g["w_mem_kv"], wout=g["w_out"],
            lng=g["ln_g"].reshape(2, 1, D), lnb=g["ln_b"].reshape(2, 1, D))
        in_maps.append({k: f(np.asarray(v)) for k, v in m.items()})
    res = run_bass_kernel_spmd(nc, in_maps, core_ids=list(range(8)))
    R = res.results
    cat = lambda k, ax: np.concatenate([np.expand_dims(r[k], ax) if False else r[k] for r in R], axis=ax)
    y_p = np.stack([r["yp"] for r in R], 0)
    y_s = np.stack([r["ys"].reshape(4, 4, D) for r in R], 0).reshape(32, 4, D)
    pstk = lambda k, shp: np.stack([r[k] for r in R], 1).reshape(shp)
    scat = lambda k, shp: np.concatenate([r[k] for r in R], 1).reshape(shp)
    return (y_p, y_s,
            pstk("p_gc", (2, 8, 3, 1536)), pstk("p_gs", (2, 8, 4, 128, 128)), pstk("p_sc", (2, 8, 3, 768)),
            pstk("p_ss", (2, 8, 8, 64, 64)), pstk("p_fk", (2, 8, T, 8, 64)), pstk("p_fv", (2, 8, T, 8, 64)),
            pstk("p_fl", (2, 8, T, 8)), pstk("p_mk", (2, 8, 256, 4, 128)), pstk("p_mv", (2, 8, 256, 4, 128)),
            scat("s_gc", (2, 32, 3, 1536)), scat("s_gs", (2, 32, 4, 128, 128)), scat("s_sc", (2, 32, 3, 768)),
            scat("s_ss", (2, 32, 8, 64, 64)), scat("s_fk", (2, 32, 4, 8, 64)), scat("s_fv", (2, 32, 4, 8, 64)),
            scat("s_fl", (2, 32, 4, 8)))
```
